# Optimizing a Trainium2 kernel written in Bass

```python
import math
import jax, jax.numpy as jnp
from jax import lax
import numpy as np

D_MODEL = 1024
BATCH = 4
SEQ = 8192
DEPTH = 1

DA_HEADS = 8
DA_HEAD_DIM = 64
DA_V_DIM = 2 * DA_HEAD_DIM
DA_QK_WIDTH = 2 * DA_HEADS * DA_HEAD_DIM
DA_WIDTH = DA_HEADS * DA_V_DIM
GLA_HEADS = 4
GLA_K_DIM = 128
GLA_V_DIM = 256
GLA_QK_WIDTH = GLA_HEADS * GLA_K_DIM
GLA_WIDTH = GLA_HEADS * GLA_V_DIM
GLA_GATE_RANK = 16
GLA_GATE_NORMALIZER = 16.0
GLA_CHUNK = 64

MIX_WIDTH = DA_WIDTH + GLA_WIDTH
Q_BLOCK = 128
NORM_EPS = 1e-6
IN_SPLITS = (DA_QK_WIDTH, DA_QK_WIDTH, DA_WIDTH, DA_WIDTH,
             GLA_QK_WIDTH, GLA_QK_WIDTH, GLA_WIDTH, GLA_WIDTH, GLA_GATE_RANK)
IN_WIDTH = sum(IN_SPLITS)

kernel_name = "hymba_diffattn_gla_sandwich"


def rms_norm(x, g):
    xf = x.astype(jnp.float32)
    y = xf * lax.rsqrt(jnp.mean(xf * xf, axis=-1, keepdims=True) + NORM_EPS)
    return (y * g.astype(jnp.float32)).astype(x.dtype)


def split_columns(z):
    parts, start = [], 0
    for w in IN_SPLITS:
        parts.append(z[..., start:start + w])
        start += w
    return parts


def diff_attention(q, k, v, lam):
    B, S = q.shape[:2]
    nb = S // Q_BLOCK
    q = q * (DA_HEAD_DIM ** -0.5)
    q_blocks = jnp.moveaxis(q.reshape(B, nb, Q_BLOCK, DA_HEADS, 2, DA_HEAD_DIM), 1, 0)
    key_pos = jnp.arange(S)

    def block(args):
        qb, bi = args
        s = jnp.einsum('bqhmd,bkhmd->bhmqk', qb, k).astype(jnp.float32)
        q_pos = bi * Q_BLOCK + jnp.arange(Q_BLOCK)
        mask = key_pos[None, :] <= q_pos[:, None]
        p = jax.nn.softmax(jnp.where(mask, s, -jnp.inf), axis=-1)
        w = p[:, :, 0] - lam * p[:, :, 1]
        return jnp.einsum('bhqk,bkhv->bqhv', w.astype(v.dtype), v)

    out = lax.map(block, (q_blocks, jnp.arange(nb)))
    return jnp.moveaxis(out, 0, 1).reshape(B, S, DA_HEADS, DA_V_DIM)


def gla_chunked(q, k, v, log_a):
    B, S, H, dk = q.shape
    dv = v.shape[-1]
    C = GLA_CHUNK
    N = S // C
    f32 = jnp.float32

    def chunks(t):
        return t.astype(f32).reshape(B, N, C, H, t.shape[-1]).transpose(0, 3, 1, 2, 4)

    q = chunks(q) * (dk ** -0.5)
    k = chunks(k)
    v = chunks(v)
    b = jnp.cumsum(chunks(log_a), axis=3)
    b_last = b[:, :, :, -1, :]
    q_dec = q * jnp.exp(b)
    k_inv = k * jnp.exp(-b)
    k_tail = k * jnp.exp(b_last[:, :, :, None, :] - b)

    causal = jnp.tril(jnp.ones((C, C), dtype=bool))
    attn = jnp.where(causal, jnp.einsum('bhnik,bhnjk->bhnij', q_dec, k_inv), 0.0)
    o_intra = jnp.einsum('bhnij,bhnjv->bhniv', attn, v)

    def step(state, xs):
        qd, kt, vc, dl = xs
        o = jnp.einsum('bhck,bhkv->bhcv', qd, state)
        state = state * dl[..., None] + jnp.einsum('bhck,bhcv->bhkv', kt, vc)
        return state, o

    xs = (jnp.moveaxis(q_dec, 2, 0), jnp.moveaxis(k_tail, 2, 0),
          jnp.moveaxis(v, 2, 0), jnp.moveaxis(jnp.exp(b_last), 2, 0))
    state0 = jnp.zeros((B, H, dk, dv), f32)
    _, o_inter = lax.scan(step, state0, xs)
    o = o_intra + jnp.moveaxis(o_inter, 0, 2)
    return o.transpose(0, 2, 3, 1, 4).reshape(B, S, H, dv)


def hybrid_layer(x, layer_idx, pre_g, post_g, w_in, w_gk_up, b_gk,
                 lq1, lk1, lq2, lk2, subln_g, gla_g, w_out):
    B, S, _ = x.shape
    h = rms_norm(x, pre_g)
    z = jnp.einsum('bsd,de->bse', h, w_in)
    qa, ka, va, ga, qb, kb, vb, gb, gk_low = split_columns(z)

    lam_init = 0.8 - 0.6 * math.exp(-0.3 * layer_idx)
    lam = (jnp.exp(jnp.sum(lq1.astype(jnp.float32) * lk1.astype(jnp.float32)))
           - jnp.exp(jnp.sum(lq2.astype(jnp.float32) * lk2.astype(jnp.float32)))
           + lam_init)
    oa = diff_attention(qa.reshape(B, S, DA_HEADS, 2, DA_HEAD_DIM),
                        ka.reshape(B, S, DA_HEADS, 2, DA_HEAD_DIM),
                        va.reshape(B, S, DA_HEADS, DA_V_DIM), lam)
    oa = rms_norm(oa, subln_g) * (1.0 - lam_init)
    oa = oa.reshape(B, S, DA_WIDTH) * jax.nn.silu(ga)

    gk = jnp.einsum('bsr,re->bse', gk_low, w_gk_up) + b_gk
    log_a = jax.nn.log_sigmoid(gk.astype(jnp.float32)) / GLA_GATE_NORMALIZER
    ob = gla_chunked(qb.reshape(B, S, GLA_HEADS, GLA_K_DIM),
                     kb.reshape(B, S, GLA_HEADS, GLA_K_DIM),
                     vb.reshape(B, S, GLA_HEADS, GLA_V_DIM),
                     log_a.reshape(B, S, GLA_HEADS, GLA_K_DIM))
    ob = rms_norm(ob, gla_g).astype(x.dtype)
    ob = ob.reshape(B, S, GLA_WIDTH) * jax.nn.silu(gb)

    y = jnp.einsum('bse,ed->bsd', jnp.concatenate([oa, ob], axis=-1), w_out)
    return x + rms_norm(y, post_g)


def setup_inputs(seed: int = 0) -> dict:
    key = jax.random.key(seed)
    ks = jax.random.split(key, 14)
    f32 = jnp.float32
    L = DEPTH
    return {
        "x": jax.random.normal(ks[0], (BATCH, SEQ, D_MODEL), f32),
        "pre_norm_g": 1.0 + 0.02 * jax.random.normal(ks[1], (L, D_MODEL), f32),
        "post_norm_g": 1.0 + 0.02 * jax.random.normal(ks[2], (L, D_MODEL), f32),
        "w_in": jax.random.normal(ks[3], (L, D_MODEL, IN_WIDTH), f32) * D_MODEL ** -0.5,
        "w_gk_up": jax.random.normal(ks[4], (L, GLA_GATE_RANK, GLA_QK_WIDTH), f32) * GLA_GATE_RANK ** -0.5,
        "b_gk": 0.01 * jax.random.normal(ks[5], (L, GLA_QK_WIDTH), f32),
        "lambda_q1": 0.1 * jax.random.normal(ks[6], (L, DA_HEAD_DIM), f32),
        "lambda_k1": 0.1 * jax.random.normal(ks[7], (L, DA_HEAD_DIM), f32),
        "lambda_q2": 0.1 * jax.random.normal(ks[8], (L, DA_HEAD_DIM), f32),
        "lambda_k2": 0.1 * jax.random.normal(ks[9], (L, DA_HEAD_DIM), f32),
        "attn_subln_g": 1.0 + 0.02 * jax.random.normal(ks[10], (L, DA_V_DIM), f32),
        "gla_norm_g": 1.0 + 0.02 * jax.random.normal(ks[11], (L, GLA_V_DIM), f32),
        "w_out": jax.random.normal(ks[12], (L, MIX_WIDTH, D_MODEL), f32) * MIX_WIDTH ** -0.5,
    }


def reference(x, pre_norm_g, post_norm_g, w_in, w_gk_up, b_gk, lambda_q1, lambda_k1,
              lambda_q2, lambda_k2, attn_subln_g, gla_norm_g, w_out):
    for l in range(DEPTH):
        x = hybrid_layer(x, l, pre_norm_g[l], post_norm_g[l], w_in[l], w_gk_up[l], b_gk[l],
                         lambda_q1[l], lambda_k1[l], lambda_q2[l], lambda_k2[l],
                         attn_subln_g[l], gla_norm_g[l], w_out[l])
    return x
```

```python
import numpy as np
from contextlib import ExitStack
import concourse.bass as bass
import concourse.mybir as mybir
from concourse.bass_utils import run_bass_kernel_spmd

F32 = mybir.dt.float32
BF16 = mybir.dt.bfloat16
AF = mybir.ActivationFunctionType
ALU = mybir.AluOpType
AX = mybir.AxisListType

D = 1024
NBLK = 64
NOWN = 32
EPS = 1e-6
LAM_INIT = 0.2
NEG = -30000.0

ENGS = ("pe", "act", "dve", "pool", "sp")


class Op:
    __slots__ = ("eng", "fn", "deps", "idx", "semkey", "val", "inc", "is_dma", "signals", "prev_val")


class Sched:
    EPOCH = 4096
    NDMA = 16

    def __init__(self, nc, name):
        self.nc = nc
        self.name = name
        self.ops = []
        self.lw = {}
        self.rd = {}

    def op(self, eng, fn, reads=(), writes=(), dma=False):
        o = Op()
        o.eng, o.fn, o.is_dma, o.idx, o.signals = eng, fn, dma, len(self.ops), False
        deps = {}
        for r in reads:
            w = self.lw.get(r)
            if w is not None:
                deps[w.idx] = w
        for wr in writes:
            w = self.lw.get(wr)
            if w is not None:
                deps[w.idx] = w
            for r in self.rd.get(wr, ()):
                deps[r.idx] = r
        o.deps = list(deps.values())
        for wr in writes:
            self.lw[wr] = o
            self.rd[wr] = []
        for r in reads:
            if r not in writes:
                self.rd.setdefault(r, []).append(o)
        self.ops.append(o)
        return o

    def run(self):
        nc = self.nc
        for o in self.ops:
            nd = []
            for d in o.deps:
                if d.eng == "pe" and o.eng == "pe" and not d.is_dma and not o.is_dma:
                    continue
                d.signals = True
                nd.append(d)
            o.deps = nd
        cnt = {}
        dn = {}
        duses = {}
        for o in self.ops:
            if o.is_dma:
                q = "d" + o.eng
                k = dn.get(q, 0) % self.NDMA
                dn[q] = dn.get(q, 0) + 1
                u = duses.get((q, k), 0)
                o.prev_val = 16 * u
                duses[(q, k)] = u + 1
                o.semkey, o.val, o.inc = (q, k), 16 * (u + 1), 16
            elif o.signals:
                c = cnt.get(o.eng, 0)
                cnt[o.eng] = c + 1
                o.semkey, o.val, o.inc = (o.eng, c // self.EPOCH), (c % self.EPOCH) + 1, 1
        keys = []
        for o in self.ops:
            if (o.is_dma or o.signals) and o.semkey not in keys:
                keys.append(o.semkey)
        per = {e: [o for o in self.ops if o.eng == e] for e in ENGS}
        if True:
            sems = {k: nc.alloc_semaphore(name=f"{self.name}_{k[0]}_{k[1]}") for k in keys}
            with nc.Block() as pre:
                @pre.sync
                def _(e):
                    for sh in sems.values():
                        e.sem_clear(sh)

            def emit(engname, e):
                known = {}
                for o in per[engname]:
                    need = {}
                    for d in o.deps:
                        if need.get(d.semkey, 0) < d.val:
                            need[d.semkey] = d.val
                    if o.is_dma and o.prev_val > 0 and need.get(o.semkey, 0) < o.prev_val:
                        need[o.semkey] = o.prev_val
                    for k, v in need.items():
                        if known.get(k, 0) >= v:
                            continue
                        e.wait_ge(sems[k], v)
                        known[k] = v
                    inst = o.fn(e)
                    if o.is_dma or o.signals:
                        inst.then_inc(sems[o.semkey], o.inc)
                for (q, k), u in duses.items():
                    if q == "d" + engname:
                        e.wait_ge(sems[(q, k)], 16 * u)

            with nc.Block() as block:
                @block.tensor
                def _(e):
                    emit("pe", e)

                @block.scalar
                def _(e):
                    emit("act", e)

                @block.vector
                def _(e):
                    emit("dve", e)

                @block.gpsimd
                def _(e):
                    emit("pool", e)

                @block.sync
                def _(e):
                    emit("sp", e)
            nc.clear_and_free_semaphores(list(sems.values()))
            nc.all_engine_barrier()


class Banks:
    def __init__(self, tiles, tag):
        self.t, self.i, self.tag = tiles, 0, tag

    def next(self):
        k = self.i % len(self.t)
        self.i += 1
        return (self.tag, k), self.t[k]


class Ctx:
    def __init__(self, nc, st, sc):
        self.nc, self.st, self.sc = nc, st, sc
        self.ev = 0

    def sb(self, name, shape, dt):
        return self.st.enter_context(self.nc.sbuf_tensor(f"{self.sc.name}_{name}", shape, dt))

    def ps(self, name, shape, dt):
        return self.st.enter_context(self.nc.psum_tensor(f"{self.sc.name}_{name}", shape, dt))

    def load(self, out, in_, reads, writes, slow=False):
        if slow:
            return self.sc.op("sp", lambda e: e.dma_start(out=out, in_=in_, allow_slow_non_contiguous=True),
                              reads, writes, dma=True)
        return self.sc.op("sp", lambda e: e.dma_start(out=out, in_=in_), reads, writes, dma=True)

    def store(self, out, in_, reads, writes):
        return self.sc.op("pool", lambda e: e.dma_start(out=out, in_=in_), reads, writes, dma=True)

    def mmgroup(self, out, pairs, reads, writes):
        n = len(pairs)

        def fn(e):
            inst = None
            for i, (l, r) in enumerate(pairs):
                inst = e.matmul(out, l, r, start=(i == 0), stop=(i == n - 1))
            return inst
        return self.sc.op("pe", fn, reads, writes)

    def mm(self, out, l, r, start, stop, reads, writes):
        return self.sc.op("pe", lambda e: e.matmul(out, l, r, start=start, stop=stop), reads, writes)

    def transposes(self, outs_ins, ident, reads, writes):
        def fn(e):
            inst = None
            for o, i in outs_ins:
                inst = e.transpose(o, i, ident)
            return inst
        return self.sc.op("pe", fn, reads, writes)

    def act(self, out, in_, func, reads, writes, bias=None, scale=None, accum_out=None):
        kw = {}
        if bias is not None:
            kw["bias"] = bias
        if scale is not None:
            kw["scale"] = scale
        if accum_out is not None:
            kw["accum_out"] = accum_out
        return self.sc.op("act", lambda e: e.activation(out, in_, func, **kw), reads, writes)

    def ts(self, out, in0, s1, s2, op0, op1, reads, writes, eng="dve"):
        if op1 is None:
            return self.sc.op(eng, lambda e: e.tensor_scalar(out, in0, s1, None, op0), reads, writes)
        return self.sc.op(eng, lambda e: e.tensor_scalar(out, in0, s1, s2, op0, op1), reads, writes)

    def tt(self, out, in0, in1, op, reads, writes, eng="dve"):
        return self.sc.op(eng, lambda e: e.tensor_tensor(out, in0, in1, op), reads, writes)

    def stt(self, out, in0, scalar, in1, op0, op1, reads, writes):
        return self.sc.op("dve", lambda e: e.scalar_tensor_tensor(out, in0, scalar, in1, op0, op1), reads, writes)

    def copy(self, out, in_, reads, writes, eng=None, scale=None):
        if eng is None:
            eng = "act" if (self.ev % 2 == 0) else "dve"
            self.ev += 1
        if eng == "act":
            if scale is None:
                return self.sc.op("act", lambda e: e.activation(out, in_, AF.Copy), reads, writes)
            return self.sc.op("act", lambda e: e.activation(out, in_, AF.Copy, scale=scale), reads, writes)
        if scale is None:
            return self.sc.op(eng, lambda e: e.tensor_copy(out, in_), reads, writes)
        return self.sc.op(eng, lambda e: e.tensor_scalar(out, in0=in_, scalar1=scale, scalar2=None, op0=ALU.mult),
                          reads, writes)

    def memset(self, ap, val, writes, eng="dve"):
        return self.sc.op(eng, lambda e: e.memset(ap, val), (), writes)

    def rstd(self, out, ss, inv_n, reads, writes):
        self.act(out, ss, AF.Ln, reads, writes, bias=EPS, scale=inv_n)
        self.act(out, out, AF.Exp, writes, writes, scale=-0.5)


def convert_weights(cx, Wb, wtag, w_dram, segs, gcol_fn, wst, ncols_max):
    k = 0
    nchunk = Wb.shape[1]
    for si, (dst, src, n) in enumerate(segs):
        for c in range(nchunk):
            slot = k % len(wst)
            k += 1
            cx.load(wst[slot][:, 0:n], w_dram[c * 128:(c + 1) * 128, src:src + n], (), [("wst", slot)])
            g = gcol_fn(c)
            if k % 2 == 0:
                cx.sc.op("act", lambda e, o=Wb[:, c, dst:dst + n], i=wst[slot][:, 0:n], g=g:
                         e.activation(o, i, AF.Copy, scale=g), [("wst", slot), "gains"], [(wtag, si, "a")])
            else:
                cx.sc.op("dve", lambda e, o=Wb[:, c, dst:dst + n], i=wst[slot][:, 0:n], g=g:
                         e.tensor_scalar(o, i, g, None, ALU.mult), [("wst", slot), "gains"], [(wtag, si, "d")])

    def regions(col):
        for si, (dst, src, n) in enumerate(segs):
            if dst <= col < dst + n:
                return [(wtag, si, "a"), (wtag, si, "d")]
        raise ValueError(col)
    return regions


def norm_a(cx, T, xs, ssq, rst, hb, junk, blk, slot):
    cx.load(xs[slot][:, :], T["xk"][blk * 128:(blk + 1) * 128, :], (), [("xs", slot)])
    cx.act(junk[:, :], xs[slot][:, :], AF.Square, [("xs", slot)], ["junk", ("ssq", slot)],
           accum_out=ssq[:, slot:slot + 1])
    cx.rstd(rst[:, slot:slot + 1], ssq[:, slot:slot + 1], 1.0 / D, [("ssq", slot)], [("rst", slot)])
    cx.ts(hb[:, :], xs[slot][:, :], rst[:, slot:slot + 1], None, ALU.mult, None,
          [("xs", slot), ("rst", slot)], ["hb"])


def norm_b(cx, hb, PT, ident, dst_hT, dst_region):
    cx.transposes([(PT[:, c * 128:(c + 1) * 128], hb[:, c * 128:(c + 1) * 128]) for c in range(8)],
                  ident[:, :], ["hb", "ident"], ["pt"])
    cx.copy(dst_hT, PT[:, :].rearrange("p (c t) -> p c t", c=8), ["pt"], [dst_region])


def norm_block(cx, T, xs, ssq, rst, hb, junk, PT, ident, blk, slot, dst_hT, dst_region, tag):
    norm_a(cx, T, xs, ssq, rst, hb, junk, blk, slot)
    norm_b(cx, hb, PT, ident, dst_hT, dst_region)


KA, VA, QB, KB, VB, GB, GK = 0, 1024, 2048, 2560, 3072, 4096, 5120
NW1 = 5136


def phase_A1(nc, T):
    with ExitStack() as st:
        sc = Sched(nc, "A1")
        cx = Ctx(nc, st, sc)
        Wb = cx.sb("Wb", [128, 8, NW1], BF16)
        wst = [cx.sb(f"wst{i}", [128, 1024], F32) for i in range(6)]
        gpre = cx.sb("gpre", [128, 8], F32)
        xs = [cx.sb(f"xs{i}", [128, D], F32) for i in range(2)]
        hb = cx.sb("hb", [128, D], BF16)
        junk = cx.sb("junk", [128, D], BF16)
        ssq = cx.sb("ssq", [128, 2], F32)
        rst = cx.sb("rst", [128, 2], F32)
        hT = [cx.sb(f"hT{i}", [128, 8, 512], BF16) for i in range(2)]
        stage_k = cx.sb("stage_k", [128, 8, 512], BF16)
        stage_v = cx.sb("stage_v", [128, 8, 512], BF16)
        mstage = cx.sb("mstage", [128, 8, 256], BF16)
        identf = cx.sb("identf", [128, 128], F32)
        ident = cx.sb("ident", [128, 128], BF16)
        trif = cx.sb("trif", [128, 128], F32)
        lstr = cx.sb("lstr", [128, 128], F32)
        tri4 = cx.sb("tri4", [128, 4, 128], BF16)
        onesf = cx.sb("onesf", [128, 2], F32)
        gkT = cx.sb("gkT", [32, 512], F32)
        waug = cx.sb("waug", [32, 512], F32)
        e_sb = cx.sb("e_sb", [128, 512], F32)
        l_sb = cx.sb("l_sb", [128, 512], F32)
        dt_sb = cx.sb("dt_sb", [128, 512], F32)
        dl_sb = cx.sb("dl_sb", [128, 4], F32)
        ktail = cx.sb("ktail", [128, 512], BF16)
        vb_bf = cx.sb("vb_bf", [128, 1024], BF16)
        state = cx.sb("state", [128, 1024], F32)
        state_bf = cx.sb("state_bf", [128, 1024], BF16)
        eneg = cx.sb("eneg", [128, 512], F32)
        epos = cx.sb("epos", [128, 512], F32)
        qdec = cx.sb("qdec", [128, 4, 128], BF16)
        kinv = cx.sb("kinv", [128, 4, 128], BF16)
        attn = cx.sb("attn", [128, 4, 128], BF16)
        sgb = cx.sb("sgb", [128, 1024], BF16)
        mixb = cx.sb("mixb", [128, 1024], BF16)
        ssq4 = cx.sb("ssq4", [128, 4], F32)
        rs4 = cx.sb("rs4", [128, 4], F32)
        pbs = Banks([cx.ps(f"pA{i}", [128, 512], F32) for i in range(7)], "ps")
        PT = cx.ps("PT", [128, 1024], BF16)

        cx.load(gpre[:, :], T["pre_g"].rearrange("o (c p) -> p (o c)", p=128), (), ["gains"], slow=True)
        cx.load(identf[:, :], T["ident"][:, :], (), ["identf"])
        cx.load(trif[:, :], T["tri"][:, :], (), ["trif"])
        cx.load(waug[0:16, :], T["w_gk"][:, :], (), ["waug"])
        cx.load(waug[16:17, :], T["b_gk"][:, :], (), ["waug"])
        cx.copy(ident[:, :], identf[:, :], ["identf"], ["ident"], eng="dve")
        cx.ts(lstr[:, :], trif[:, :], -1.0, 1.0, ALU.mult, ALU.add, ["trif"], ["lstr"])
        for h in range(4):
            cx.copy(tri4[:, h, :], trif[:, :], ["trif"], ["tri4"], eng="dve")
        cx.memset(onesf[:, :], 1.0, ["onesf"])
        cx.memset(gkT[:, :], 1.0, ["gkT"])
        cx.memset(state[:, :], 0.0, ["state"])
        cx.memset(state_bf[:, :], 0.0, ["state_bf"])
        segs = [(KA, 1024, 1024), (VA, 2048, 1024), (GK, 7168, 16), (QB, 4096, 1024), (VB, 5120, 1024),
                (GB, 6144, 1024)]
        for bi in range(4):
            norm_block(cx, T, xs, ssq, rst, hb, junk, PT, ident, bi, bi % 2,
                       hT[0][:, :, bi * 128:(bi + 1) * 128], ("hT", 0), "A1")
        WR = convert_weights(cx, Wb, "Wb", T["w_in"], segs, lambda c: gpre[:, c:c + 1], wst, 1040)

        NS = NBLK // 4

        def na(s, bi):
            blk = 4 * s + bi
            norm_a(cx, T, xs, ssq, rst, hb, junk, blk, blk % 2)

        def nb(s, bi):
            norm_b(cx, hb, PT, ident, hT[s % 2][:, :, bi * 128:(bi + 1) * 128], ("hT", s % 2))

        def norm_sb(s, bi):
            na(s, bi)
            nb(s, bi)

        deferred = []

        def flush():
            while deferred:
                deferred.pop(0)()

        for s in range(NS):
            hs = s % 2
            hTs = hT[hs]
            HR = ("hT", hs)
            nxt = s + 1 < NS
            if nxt:
                na(s + 1, 0)
            for h in range(8):
                pk, p = pbs.next()
                cx.mmgroup(p[:, :], [(Wb[:, c, KA + h * 128:KA + (h + 1) * 128], hTs[:, c, :]) for c in range(8)],
                           WR(KA) + [HR], [pk])
                cx.copy(stage_k[:, h, :], p[:, :], [pk], ["stage_k"])
                if h == 3:
                    flush()
            cx.store(T["KT"][:, :, s * 512:(s + 1) * 512].rearrange("h p n -> p h n"), stage_k[:, :, :],
                     ["stage_k"], ["KTd"])
            if nxt:
                nb(s + 1, 0)
                na(s + 1, 1)
            for bi in range(4):
                for half in range(2):
                    pk, p = pbs.next()
                    cx.mmgroup(p[:, :], [(hTs[:, c, bi * 128:(bi + 1) * 128],
                                          Wb[:, c, VA + half * 512:VA + (half + 1) * 512]) for c in range(8)],
                               WR(VA) + [HR], [pk])
                    cx.copy(stage_v[:, half * 4:(half + 1) * 4, bi * 128:(bi + 1) * 128],
                            p[:, :].rearrange("p (h v) -> p h v", h=4), [pk], ["stage_v"])
                if nxt and bi == 1:
                    nb(s + 1, 1)
            cx.store(T["Vs"][:, :, 4 * s:4 * s + 4, :].rearrange("h p i v -> p h (i v)"), stage_v[:, :, :],
                     ["stage_v"], ["Vd"])
            pk, p = pbs.next()
            cx.mmgroup(p[0:16, :], [(Wb[:, c, GK:GK + 16], hTs[:, c, :]) for c in range(8)], WR(GK) + [HR], [pk])
            cx.copy(gkT[0:16, :], p[0:16, :], [pk], ["gkT"], eng="dve")
            for bi in range(4):
                own = (bi % 2 == 1)
                oi = bi // 2
                tk = slice(bi * 128, (bi + 1) * 128)
                if nxt and own:
                    na(s + 1, 2 + oi)
                pk, p = pbs.next()
                cx.mm(p[:, :], gkT[0:17, tk], waug[0:17, :], True, True, ["gkT", "waug"], [pk])
                cx.act(e_sb[:, :], p[:, :], AF.Exp, [pk], ["e_sb"], scale=-1.0)
                cx.act(l_sb[:, :], e_sb[:, :], AF.Ln, ["e_sb"], ["l_sb"], bias=1.0)
                pkb, pkbt = pbs.next()
                cx.mmgroup(pkbt[:, :], [(hTs[:, c, tk], Wb[:, c, KB:KB + 512]) for c in range(8)], WR(KB) + [HR], [pkb])
                for half in range(2):
                    pk, p = pbs.next()
                    cx.mmgroup(p[:, :], [(hTs[:, c, tk], Wb[:, c, VB + half * 512:VB + (half + 1) * 512])
                                         for c in range(8)], WR(VB) + [HR], [pk])
                    cx.copy(vb_bf[:, half * 512:(half + 1) * 512], p[:, :], [pk], ["vb_bf"])
                flush()
                pk, p = pbs.next()
                cx.mm(p[:, :], lstr[:, :], l_sb[:, :], True, True, ["lstr", "l_sb"], [pk])
                cx.act(dt_sb[:, :], p[:, :], AF.Exp, [pk], ["dt_sb"], scale=-1.0 / 16.0)
                cx.tt(ktail[:, :], pkbt[:, :], dt_sb[:, :], ALU.mult, [pkb, "dt_sb"], ["ktail"])
                pk, p = pbs.next()
                for h in range(4):
                    cx.mm(p[:, h:h + 1], l_sb[:, h * 128:(h + 1) * 128], onesf[:, 0:1], True, True,
                          ["l_sb", "onesf"], [pk])
                cx.act(dl_sb[:, :], p[:, 0:4], AF.Exp, [pk], ["dl_sb"], scale=-1.0 / 16.0)
                if own:
                    pkc, pc = pbs.next()
                    for h in range(4):
                        cx.mm(pc[:, h * 128:(h + 1) * 128], l_sb[:, h * 128:(h + 1) * 128], trif[:, :], True, True,
                              ["l_sb", "trif"], [pkc])
                    cx.act(eneg[:, :], pc[:, :], AF.Exp, [pkc], ["eneg"], scale=-1.0 / 16.0)
                    cx.act(epos[:, :], pc[:, :], AF.Exp, [pkc], ["epos"], scale=1.0 / 16.0)
                    pkq, pq = pbs.next()
                    for h in range(4):
                        cx.mmgroup(pq[:, h * 128:(h + 1) * 128],
                                   [(Wb[:, c, QB + h * 128:QB + (h + 1) * 128], hTs[:, c, tk]) for c in range(8)],
                                   WR(QB) + [HR], [pkq])
                    cx.stt(qdec[:, :, :].rearrange("p h t -> p (h t)"), pq[:, :], 128.0 ** -0.5, eneg[:, :],
                           ALU.mult, ALU.mult, [pkq, "eneg"], ["qdec"])
                    pkk, pkt = pbs.next()
                    for h in range(4):
                        cx.mmgroup(pkt[:, h * 128:(h + 1) * 128],
                                   [(Wb[:, c, KB + h * 128:KB + (h + 1) * 128], hTs[:, c, tk]) for c in range(8)],
                                   WR(KB) + [HR], [pkk])
                    cx.tt(kinv[:, :, :].rearrange("p h t -> p (h t)"), pkt[:, :], epos[:, :], ALU.mult,
                          [pkk, "epos"], ["kinv"])
                    for half in range(2):
                        pk, p = pbs.next()
                        cx.mmgroup(p[:, :], [(hTs[:, c, tk], Wb[:, c, GB + half * 512:GB + (half + 1) * 512])
                                             for c in range(8)], WR(GB) + [HR], [pk])
                        cx.act(sgb[:, half * 512:(half + 1) * 512], p[:, :], AF.Silu, [pk], ["sgb"])
                    pka, pa = pbs.next()
                    for h in range(4):
                        cx.mm(pa[:, h * 128:(h + 1) * 128], kinv[:, h, :], qdec[:, h, :], True, True,
                              ["kinv", "qdec"], [pka])
                    cx.tt(attn[:, :, :].rearrange("p h t -> p (h t)"), pa[:, :],
                          tri4[:, :, :].rearrange("p h t -> p (h t)"), ALU.mult, [pka, "tri4"], ["attn"])
                pks = []
                for hp in range(2):
                    pk, p = pbs.next()
                    pks.append((pk, p))
                    for hh in range(2):
                        h = hp * 2 + hh
                        cx.mm(p[:, hh * 256:(hh + 1) * 256], ktail[:, h * 128:(h + 1) * 128],
                              vb_bf[:, h * 256:(h + 1) * 256], True, True, ["ktail", "vb_bf"], [pk])
                if own:
                    pko = []
                    for hp in range(2):
                        pk, p = pbs.next()
                        pko.append((pk, p))
                        for hh in range(2):
                            h = hp * 2 + hh
                            cx.mmgroup(p[:, hh * 256:(hh + 1) * 256],
                                       [(attn[:, h, :], vb_bf[:, h * 256:(h + 1) * 256]),
                                        (qdec[:, h, :], state_bf[:, h * 256:(h + 1) * 256])],
                                       ["attn", "vb_bf", "qdec", "state_bf"], [pk])
                    for h in range(4):
                        pk, p = pko[h // 2]
                        cx.act(junk[:, 0:256], p[:, (h % 2) * 256:(h % 2 + 1) * 256], AF.Square, [pk],
                               ["junk", "ssq4"], accum_out=ssq4[:, h:h + 1])
                    cx.rstd(rs4[:, :], ssq4[:, :], 1.0 / 256.0, ["ssq4"], ["rs4"])
                    for h in range(4):
                        pk, p = pko[h // 2]
                        cx.stt(mixb[:, h * 256:(h + 1) * 256], p[:, (h % 2) * 256:(h % 2 + 1) * 256],
                               rs4[:, h:h + 1], sgb[:, h * 256:(h + 1) * 256], ALU.mult, ALU.mult,
                               [pk, "rs4", "sgb"], ["mixb"])

                    def epilogue(oi=oi, s=s):
                        cx.transposes([(PT[:, c * 128:(c + 1) * 128], mixb[:, c * 128:(c + 1) * 128])
                                       for c in range(8)], ident[:, :], ["mixb", "ident"], ["pt"])
                        cx.copy(mstage[:, :, oi * 128:(oi + 1) * 128],
                                PT[:, :].rearrange("p (c t) -> p c t", c=8), ["pt"], ["mstage"])
                        if oi == 1:
                            t0 = 2 * s
                            cx.store(T["MT"][8:16, :, t0 * 128:(t0 + 2) * 128].rearrange("c p n -> p c n"),
                                     mstage[:, :, :], ["mstage"], ["MTd"])
                    deferred.append(epilogue)
                for hp in range(2):
                    pk, p = pks[hp]
                    for hh in range(2):
                        h = hp * 2 + hh
                        cx.stt(state[:, h * 256:(h + 1) * 256], state[:, h * 256:(h + 1) * 256], dl_sb[:, h:h + 1],
                               p[:, hh * 256:(hh + 1) * 256], ALU.mult, ALU.add, [pk, "dl_sb", "state"], ["state"])
                cx.copy(state_bf[:, :], state[:, :], ["state"], ["state_bf"], eng="act")
                if nxt and own:
                    nb(s + 1, 2 + oi)
        flush()
        sc.run()


def phase_A2(nc, T):
    with ExitStack() as st:
        sc = Sched(nc, "A2")
        cx = Ctx(nc, st, sc)
        Wb = cx.sb("Wq", [128, 8, 2048], BF16)
        wst = [cx.sb(f"wst{i}", [128, 1024], F32) for i in range(6)]
        gpre = cx.sb("gpre", [128, 8], F32)
        xs = [cx.sb(f"xs{i}", [128, D], F32) for i in range(2)]
        hb = cx.sb("hb", [128, D], BF16)
        junk = cx.sb("junk", [128, D], BF16)
        ssq = cx.sb("ssq", [128, 2], F32)
        rst = cx.sb("rst", [128, 2], F32)
        hT = [cx.sb(f"hT{i}", [128, 8, 512], BF16) for i in range(2)]
        stq = [cx.sb(f"stq{i}", [128, 8, 512], BF16) for i in range(2)]
        identf = cx.sb("identf", [128, 128], F32)
        ident = cx.sb("ident", [128, 128], BF16)
        pbs = Banks([cx.ps(f"pB{i}", [128, 512], F32) for i in range(6)], "ps")
        PT = cx.ps("PT", [128, 1024], BF16)
        cx.load(gpre[:, :], T["pre_g"].rearrange("o (c p) -> p (o c)", p=128), (), ["gains"], slow=True)
        cx.load(identf[:, :], T["ident"][:, :], (), ["identf"])
        cx.copy(ident[:, :], identf[:, :], ["identf"], ["ident"], eng="dve")
        for bi in range(4):
            norm_block(cx, T, xs, ssq, rst, hb, junk, PT, ident, 2 * bi + 1, bi % 2,
                       hT[0][:, :, bi * 128:(bi + 1) * 128], ("hT", 0), "A2")
        WR = convert_weights(cx, Wb, "Wq", T["w_in"], [(0, 0, 1024), (1024, 3072, 1024)],
                             lambda c: gpre[:, c:c + 1], wst, 1024)
        NG2 = NOWN // 4

        def na2(g, bi):
            t = 4 * g + bi
            norm_a(cx, T, xs, ssq, rst, hb, junk, 2 * t + 1, t % 2)

        def nb2(g, bi):
            norm_b(cx, hb, PT, ident, hT[g % 2][:, :, bi * 128:(bi + 1) * 128], ("hT", g % 2))

        for g in range(NG2):
            hs = g % 2
            hTs = hT[hs]
            HR = ("hT", hs)
            nxt = g + 1 < NG2
            if nxt:
                na2(g + 1, 0)
            for which, dst, sreg in ((0, "QT", ("stq", 0)), (1, "GT", ("stq", 1))):
                for h in range(8):
                    pk, p = pbs.next()
                    col = which * 1024 + h * 128
                    cx.mmgroup(p[:, :], [(Wb[:, c, col:col + 128], hTs[:, c, :]) for c in range(8)], WR(col) + [HR], [pk])
                    if which == 0:
                        cx.copy(stq[0][:, h, :], p[:, :], [pk], [sreg], scale=0.125)
                    else:
                        cx.act(stq[1][:, h, :], p[:, :], AF.Silu, [pk], [sreg])
                    if nxt and h % 4 == 3:
                        j = which * 2 + h // 4
                        nb2(g + 1, j)
                        if j + 1 < 4:
                            na2(g + 1, j + 1)
                cx.store(T[dst][:, :, g * 512:(g + 1) * 512].rearrange("h p n -> p h n"), stq[which][:, :, :],
                         [sreg], [dst + "d"])
        sc.run()


def phase_B(nc, T):
    with ExitStack() as st:
        sc = Sched(nc, "B")
        cx = Ctx(nc, st, sc)
        NT = NBLK * 128
        NQ = NOWN * 128
        KTs = [cx.sb(f"KTs{i}", [128, NT], BF16) for i in range(2)]
        Vh = [cx.sb(f"Vh{i}", [128, NT], BF16) for i in range(2)]
        QTs = [cx.sb(f"QTs{i}", [128, NQ], BF16) for i in range(2)]
        GTs = [cx.sb(f"GTs{i}", [128, NQ], BF16) for i in range(2)]
        Pt = [[cx.sb(f"Pt{i}_{m}", [128, 512], BF16) for m in range(2)] for i in range(3)]
        acc1 = cx.sb("acc1", [128, 512], F32)
        trif = cx.sb("trif", [128, 128], F32)
        tri = cx.sb("tri", [128, 128], BF16)
        ones_bf = cx.sb("ones_bf", [128, 128], BF16)
        ones_f = cx.sb("ones_f", [128, 128], F32)
        kb0 = cx.sb("kb0", [128, 1], F32)
        lamt = cx.sb("lamt", [128, 256], F32)
        lprod = cx.sb("lprod", [128, 128], F32)
        lsum = cx.sb("lsum", [128, 2], F32)
        lexp = cx.sb("lexp", [128, 2], F32)
        neglam = cx.sb("neglam", [128, 1], F32)
        rc = [cx.sb(f"rc{m}", [128, 512], F32) for m in range(2)]
        nm = [cx.sb(f"nm{m}", [128, 512], F32) for m in range(2)]
        oa = cx.sb("oa", [128, 512], F32)
        sq = cx.sb("sq", [128, 512], F32)
        rsd = cx.sb("rsd", [128, 512], F32)
        mst = [cx.sb(f"mst{i}", [128, 512], BF16) for i in range(2)]
        sbk = Banks([cx.ps(f"pS{i}", [128, 512], F32) for i in range(4)], "ps")
        OT = [cx.ps(f"OT{m}", [128, 512], F32) for m in range(2)]
        DT = [cx.ps(f"DT{m}", [128, 512], F32) for m in range(2)]

        cx.load(trif[:, :], T["tri"][:, :], (), ["trif"])
        cx.load(kb0[:, :], T["kb0"][:, :], (), ["kb0"])
        cx.load(lamt[:, :], T["lam"].rearrange("a b -> (a b)").partition_broadcast(128), (), ["lamt"])
        cx.copy(tri[:, :], trif[:, :], ["trif"], ["tri"], eng="dve")
        cx.memset(ones_bf[:, :], 1.0, ["ones_bf"])
        cx.memset(ones_f[:, :], 1.0, ["ones_f"])
        cx.tt(lprod[:, 0:64], lamt[:, 0:64], lamt[:, 64:128], ALU.mult, ["lamt"], ["lprod"])
        cx.tt(lprod[:, 64:128], lamt[:, 128:192], lamt[:, 192:256], ALU.mult, ["lamt"], ["lprod"])
        cx.sc.op("dve", lambda e: e.reduce_sum(lsum[:, :], lprod[:, :].rearrange("p (a b) -> p a b", a=2), AX.X),
                 ["lprod"], ["lsum"])
        cx.act(lexp[:, :], lsum[:, :], AF.Exp, ["lsum"], ["lexp"])
        cx.tt(neglam[:, :], lexp[:, 1:2], lexp[:, 0:1], ALU.subtract, ["lexp"], ["neglam"])
        cx.ts(neglam[:, :], neglam[:, :], -LAM_INIT, None, ALU.add, None, ["neglam"], ["neglam"])

        OTs = [cx.sb(f"OTs{m}", [128, 512], F32) for m in range(2)]
        DTs = [cx.sb(f"DTs{m}", [128, 512], F32) for m in range(2)]
        NG = NOWN // 4
        steps = [(h, g, kb) for h in range(8) for g in range(NG) for kb in range(8 * g + 8)]

        def loads(h):
            sl = h % 2
            for q in range(4):
                cs = slice(q * (NT // 4), (q + 1) * (NT // 4))
                cx.load(KTs[sl][:, cs], T["KT"][h, :, cs], (), [("KTs", sl)])
                cx.load(Vh[sl][:, cs], T["Vs"][h].rearrange("p i v -> p (i v)")[:, cs], (), [("Vh", sl)])
            cx.load(QTs[sl][:, :], T["QT"][h, :, :], (), [("QTs", sl)])
            cx.load(GTs[sl][:, :], T["GT"][h, :, :], (), [("GTs", sl)])

        def geom(g, kb):
            d = kb - (8 * g + 1)
            qs = 0 if d <= 0 else (d + 1) // 2
            return qs, (d >= 0 and d % 2 == 0), slice(qs * 128, 512)

        def front(i):
            h, g, kb = steps[i]
            sl = h % 2
            qs, diag, cols = geom(g, kb)
            for m in range(2):
                P = Pt[i % 3][m]
                PR = ("Pt", i % 3, m)
                pk, p = sbk.next()
                rows = slice(m * 64, (m + 1) * 64)
                cx.mm(p[:, cols], KTs[sl][rows, kb * 128:(kb + 1) * 128],
                      QTs[sl][rows, g * 512 + qs * 128:(g + 1) * 512], True, True,
                      [("KTs", sl), ("QTs", sl)], [pk])
                if kb == 0:
                    cx.act(P[:, cols], p[:, cols], AF.Exp, [pk, "kb0"], [PR], bias=kb0[:, 0:1])
                else:
                    cx.act(P[:, cols], p[:, cols], AF.Exp, [pk], [PR])
                if diag:
                    dc = slice(qs * 128, (qs + 1) * 128)
                    cx.tt(P[:, dc], P[:, dc], tri[:, :], ALU.mult, [PR, "tri"], [PR])

        def back(i):
            h, g, kb = steps[i]
            sl = h % 2
            nkb = 8 * g + 8
            qs, diag, cols = geom(g, kb)
            for m in range(2):
                P = Pt[i % 3][m]
                PR = ("Pt", i % 3, m)
                cx.mm(OT[m][:, cols], Vh[sl][:, kb * 128:(kb + 1) * 128], P[:, cols],
                      kb == 0, kb == nkb - 1, [("Vh", sl), PR], [("OT", m)])
                if m == 0:
                    cx.mm(DT[m][:, cols], ones_bf[:, :], P[:, cols],
                          kb == 0, kb == nkb - 1, ["ones_bf", PR], [("DT", m)])
                elif kb == 0:
                    cx.copy(acc1[:, cols], P[:, cols], [PR], ["acc1"], eng="dve")
                else:
                    cx.tt(acc1[:, cols], acc1[:, cols], P[:, cols], ALU.add, [PR, "acc1"], ["acc1"])

        gcount = [0]

        def gend_now(h, g):
            cx.mm(DT[1][:, :], ones_f[:, :], acc1[:, :], True, True, ["ones_f", "acc1"], [("DT", 1)])
            for m in range(2):
                cx.copy(DTs[m][:, :], DT[m][:, :], [("DT", m)], [("DTs", m)], eng="dve")
                cx.copy(OTs[m][:, :], OT[m][:, :], [("OT", m)], [("OTs", m)], eng="dve")

        def gend_ops(h, g):
            sl = h % 2
            ops = []
            for m in range(2):
                for q in range(4):
                    c = slice(q * 128, (q + 1) * 128)
                    ops.append(lambda m=m, c=c: cx.sc.op(
                        "dve", lambda e, o=rc[m][:, c], i_=DTs[m][:, c]: e.reciprocal(o, i_),
                        [("DTs", m)], [("rc", m)]))
                ops.append(lambda m=m: cx.tt(nm[m][:, :], OTs[m][:, :], rc[m][:, :], ALU.mult,
                                             [("OTs", m), ("rc", m)], [("nm", m)], eng="pool"))
            ops.append(lambda: cx.stt(oa[:, :], nm[1][:, :], neglam[:, 0:1], nm[0][:, :], ALU.mult, ALU.add,
                                      [("nm", 0), ("nm", 1), "neglam"], ["oa"]))
            ops.append(lambda: cx.tt(sq[:, :], oa[:, :], oa[:, :], ALU.mult, ["oa"], ["sq"], eng="pool"))
            ops.append(None)
            ops.append(None)

            def norm_mm():
                pk, p = sbk.next()
                cx.mm(p[:, :], ones_f[:, :], sq[:, :], True, True, ["ones_f", "sq"], [pk])
                cx.rstd(rsd[:, :], p[:, :], 1.0 / 128.0, [pk], ["rsd"])
            ops.append(norm_mm)
            ops.append(None)
            ops.append(lambda: cx.tt(oa[:, :], oa[:, :], rsd[:, :], ALU.mult, ["oa", "rsd"], ["oa"], eng="pool"))

            def fin():
                ms = gcount[0] % 2
                gcount[0] += 1
                cx.tt(mst[ms][:, :], oa[:, :], GTs[sl][:, g * 512:(g + 1) * 512], ALU.mult,
                      ["oa", ("GTs", sl)], [("mst", ms)], eng="pool")
                cx.store(T["MT"][h, :, g * 512:(g + 1) * 512], mst[ms][:, :], [("mst", ms)], ["MTd"])
            ops.append(fin)
            return ops

        pending = []
        loads(0)
        loads(1)
        front(0)
        if len(steps) > 1:
            front(1)
        for i in range(len(steps)):
            h, g, kb = steps[i]
            if i + 2 < len(steps):
                if steps[i + 2][0] != steps[i + 1][0]:
                    while pending:
                        pending.pop(0)[1]()
                front(i + 2)
            back(i)
            while pending and pending[0][0] <= i:
                pending.pop(0)[1]()
            if kb == 8 * g + 7:
                while pending:
                    pending.pop(0)[1]()
                gend_now(h, g)
                k = 0
                for fn in gend_ops(h, g):
                    k += 1
                    if fn is not None:
                        pending.append((i + k, fn))
                if g == NG - 1 and h + 2 < 8:
                    pending.append((i + k, lambda h=h: loads(h + 2)))
        while pending:
            pending.pop(0)[1]()
        sc.run()


def phase_C(nc, T):
    with ExitStack() as st:
        sc = Sched(nc, "C")
        cx = Ctx(nc, st, sc)
        Wo = cx.sb("Wo", [128, 16, 1024], BF16)
        wst = [cx.sb(f"wst{i}", [128, 1024], F32) for i in range(6)]
        gsub = cx.sb("gsub", [128, 1], F32)
        ggla = cx.sb("ggla", [128, 2], F32)
        postg = cx.sb("postg", [128, D], F32)
        mixT = [cx.sb(f"mixT{i}", [128, 16, 512], BF16) for i in range(2)]
        xr = [cx.sb(f"xr{i}", [128, D], F32) for i in range(2)]
        ot = [cx.sb(f"ot{i}", [128, D], F32) for i in range(2)]
        junk = cx.sb("junk", [128, 512], BF16)
        ss2 = cx.sb("ss2", [128, 2], F32)
        ss1 = cx.sb("ss1", [128, 1], F32)
        rs1 = cx.sb("rs1", [128, 1], F32)
        pbs = Banks([cx.ps(f"pC{i}", [128, 512], F32) for i in range(6)], "ps")

        cx.load(gsub[:, :], T["subln_g"].rearrange("o p -> p o"), (), ["gains"], slow=True)
        cx.load(ggla[:, :], T["gla_g"].rearrange("o (c p) -> p (o c)", p=128), (), ["gains"], slow=True)
        cx.load(postg[:, :], T["post_g"].rearrange("o d -> (o d)").partition_broadcast(128), (), ["postg"])
        cx.ts(gsub[:, :], gsub[:, :], 1.0 - LAM_INIT, None, ALU.mult, None, ["gains"], ["gains"])
        def load_mix(g):
            for half in range(2):
                cx.load(mixT[g % 2][:, half * 8:(half + 1) * 8, :],
                        T["MT"][half * 8:(half + 1) * 8, :, g * 512:(g + 1) * 512].rearrange("c p n -> p c n"),
                        (), [("mixT", g % 2)])

        load_mix(0)
        WR = convert_weights(cx, Wo, "Wo", T["w_out"], [(0, 0, 512), (512, 512, 512)],
                             lambda c: (gsub[:, 0:1] if c < 8 else ggla[:, (c % 2):(c % 2) + 1]), wst, 1024)
        for g in range(NOWN // 4):
            sl = g % 2
            MR = ("mixT", sl)
            if g + 1 < NOWN // 4 and g >= 1:
                load_mix(g + 1)
            if g == 0 and NOWN // 4 > 1:
                load_mix(1)
            for tb in range(4):
                t = 4 * g + tb
                xsl = t % 2
                cx.load(xr[xsl][:, :], T["xk"][(2 * t + 1) * 128:(2 * t + 2) * 128, :], (), [("xr", xsl)])
                pks = []
                for half in range(2):
                    pk, p = pbs.next()
                    pks.append((pk, p))
                    cx.mmgroup(p[:, :], [(mixT[sl][:, c, tb * 128:(tb + 1) * 128],
                                          Wo[:, c, half * 512:(half + 1) * 512]) for c in range(16)],
                               WR(half * 512) + [MR], [pk])
                    cx.act(junk[:, :], p[:, :], AF.Square, [pk], ["junk", "ss2"], accum_out=ss2[:, half:half + 1])
                cx.tt(ss1[:, :], ss2[:, 0:1], ss2[:, 1:2], ALU.add, ["ss2"], ["ss1"])
                cx.rstd(rs1[:, :], ss1[:, :], 1.0 / D, ["ss1"], ["rs1"])
                OR = ("ot", xsl)
                for half in range(2):
                    pk, p = pks[half]
                    hc = slice(half * 512, (half + 1) * 512)
                    cx.stt(ot[xsl][:, hc], p[:, :], rs1[:, 0:1], postg[:, hc], ALU.mult, ALU.mult,
                           [pk, "rs1", "postg"], [OR])
                cx.tt(ot[xsl][:, :], ot[xsl][:, :], xr[xsl][:, :], ALU.add, [OR, ("xr", xsl)], [OR])
                cx.store(T["out"][t * 128:(t + 1) * 128, :], ot[xsl][:, :], [OR], ["outd"])
        sc.run()


def build_program():
    nc = bass.Bass("TRN2", target_bir_lowering=False)
    T = {}

    def din(name, shape):
        T[name] = nc.dram_tensor(name, shape, F32, kind="ExternalInput").ap()

    din("xk", [NBLK * 128, D])
    din("w_in", [D, 7184])
    din("w_out", [2048, D])
    din("w_gk", [16, 512])
    din("b_gk", [1, 512])
    din("pre_g", [1, D])
    din("post_g", [1, D])
    din("subln_g", [1, 128])
    din("gla_g", [1, 256])
    din("lam", [4, 64])
    din("kb0", [128, 1])
    din("tri", [128, 128])
    din("ident", [128, 128])
    T["out"] = nc.dram_tensor("out", [NOWN * 128, D], F32, kind="ExternalOutput").ap()
    T["KT"] = nc.dram_tensor("KT", [8, 128, NBLK * 128], BF16, kind="Internal").ap()
    T["Vs"] = nc.dram_tensor("Vs", [8, 128, NBLK, 128], BF16, kind="Internal").ap()
    T["QT"] = nc.dram_tensor("QT", [8, 128, NOWN * 128], BF16, kind="Internal").ap()
    T["GT"] = nc.dram_tensor("GT", [8, 128, NOWN * 128], BF16, kind="Internal").ap()
    T["MT"] = nc.dram_tensor("MT", [16, 128, NOWN * 128], BF16, kind="Internal").ap()
    phase_A1(nc, T)
    phase_A2(nc, T)
    phase_B(nc, T)
    phase_C(nc, T)
    return nc


def kernel(x, pre_norm_g, post_norm_g, w_in, w_gk_up, b_gk, lambda_q1, lambda_k1, lambda_q2, lambda_k2,
           attn_subln_g, gla_norm_g, w_out):
    f = np.float32
    x = np.asarray(x, f)
    B = x.shape[0]
    common = {
        "w_in": np.ascontiguousarray(np.asarray(w_in, f)[0]),
        "w_out": np.ascontiguousarray(np.asarray(w_out, f)[0]),
        "w_gk": np.ascontiguousarray(np.asarray(w_gk_up, f)[0]),
        "b_gk": np.asarray(b_gk, f).reshape(1, 512),
        "pre_g": np.asarray(pre_norm_g, f).reshape(1, D),
        "post_g": np.asarray(post_norm_g, f).reshape(1, D),
        "subln_g": np.asarray(attn_subln_g, f).reshape(1, 128),
        "gla_g": np.asarray(gla_norm_g, f).reshape(1, 256),
        "lam": np.stack([np.asarray(v, f).reshape(64) for v in (lambda_q1, lambda_k1, lambda_q2, lambda_k2)]),
        "tri": np.triu(np.ones((128, 128), f)),
        "ident": np.eye(128, dtype=f),
    }
    in_maps = []
    for b in range(B):
        xa = np.concatenate([np.zeros((128, D), f), x[b, :63 * 128]], axis=0)
        in_maps.append(dict(common, xk=np.ascontiguousarray(xa), kb0=np.full((128, 1), NEG, f)))
        in_maps.append(dict(common, xk=np.ascontiguousarray(x[b]), kb0=np.zeros((128, 1), f)))
    nc = build_program()
    res = run_bass_kernel_spmd(nc, in_maps, core_ids=list(range(2 * B)))
    out = np.empty((B, 64, 128, D), f)
    for b in range(B):
        out[b, 0::2] = np.asarray(res.results[2 * b]["out"], f).reshape(NOWN, 128, D)
        out[b, 1::2] = np.asarray(res.results[2 * b + 1]["out"], f).reshape(NOWN, 128, D)
    return out.reshape(B, 64 * 128, D)
```

```python
import numpy as np
from contextlib import ExitStack
import concourse.bass as bass
import concourse.mybir as mybir
from concourse.bass_utils import run_bass_kernel_spmd

F32 = mybir.dt.float32
BF16 = mybir.dt.bfloat16
AF = mybir.ActivationFunctionType
ALU = mybir.AluOpType
AX = mybir.AxisListType

D = 1024
NBLK = 64
NOWN = 32
EPS = 1e-6
LAM_INIT = 0.2
NEG = -30000.0

ENGS = ("pe", "act", "dve", "pool", "sp")


class Op:
    __slots__ = ("eng", "fn", "deps", "idx", "semkey", "val", "inc", "is_dma", "signals", "prev_val")


class Sched:
    EPOCH = 4096
    NDMA = 16

    def __init__(self, nc, name):
        self.nc = nc
        self.name = name
        self.ops = []
        self.lw = {}
        self.rd = {}

    def op(self, eng, fn, reads=(), writes=(), dma=False):
        o = Op()
        o.eng, o.fn, o.is_dma, o.idx, o.signals = eng, fn, dma, len(self.ops), False
        deps = {}
        for r in reads:
            w = self.lw.get(r)
            if w is not None:
                deps[w.idx] = w
        for wr in writes:
            w = self.lw.get(wr)
            if w is not None:
                deps[w.idx] = w
            for r in self.rd.get(wr, ()):
                deps[r.idx] = r
        o.deps = list(deps.values())
        for wr in writes:
            self.lw[wr] = o
            self.rd[wr] = []
        for r in reads:
            if r not in writes:
                self.rd.setdefault(r, []).append(o)
        self.ops.append(o)
        return o

    def run(self):
        nc = self.nc
        for o in self.ops:
            nd = []
            for d in o.deps:
                if d.eng == "pe" and o.eng == "pe" and not d.is_dma and not o.is_dma:
                    continue
                d.signals = True
                nd.append(d)
            o.deps = nd
        cnt = {}
        dn = {}
        duses = {}
        for o in self.ops:
            if o.is_dma:
                q = "d" + o.eng
                k = dn.get(q, 0) % self.NDMA
                dn[q] = dn.get(q, 0) + 1
                u = duses.get((q, k), 0)
                o.prev_val = 16 * u
                duses[(q, k)] = u + 1
                o.semkey, o.val, o.inc = (q, k), 16 * (u + 1), 16
            elif o.signals:
                c = cnt.get(o.eng, 0)
                cnt[o.eng] = c + 1
                o.semkey, o.val, o.inc = (o.eng, c // self.EPOCH), (c % self.EPOCH) + 1, 1
        keys = []
        for o in self.ops:
            if (o.is_dma or o.signals) and o.semkey not in keys:
                keys.append(o.semkey)
        per = {e: [o for o in self.ops if o.eng == e] for e in ENGS}
        if True:
            sems = {k: nc.alloc_semaphore(name=f"{self.name}_{k[0]}_{k[1]}") for k in keys}
            with nc.Block() as pre:
                @pre.sync
                def _(e):
                    for sh in sems.values():
                        e.sem_clear(sh)

            def emit(engname, e):
                known = {}
                for o in per[engname]:
                    need = {}
                    for d in o.deps:
                        if need.get(d.semkey, 0) < d.val:
                            need[d.semkey] = d.val
                    if o.is_dma and o.prev_val > 0 and need.get(o.semkey, 0) < o.prev_val:
                        need[o.semkey] = o.prev_val
                    for k, v in need.items():
                        if known.get(k, 0) >= v:
                            continue
                        e.wait_ge(sems[k], v)
                        known[k] = v
                    inst = o.fn(e)
                    if o.is_dma or o.signals:
                        inst.then_inc(sems[o.semkey], o.inc)
                for (q, k), u in duses.items():
                    if q == "d" + engname:
                        e.wait_ge(sems[(q, k)], 16 * u)

            with nc.Block() as block:
                @block.tensor
                def _(e):
                    emit("pe", e)

                @block.scalar
                def _(e):
                    emit("act", e)

                @block.vector
                def _(e):
                    emit("dve", e)

                @block.gpsimd
                def _(e):
                    emit("pool", e)

                @block.sync
                def _(e):
                    emit("sp", e)
            nc.clear_and_free_semaphores(list(sems.values()))
            nc.all_engine_barrier()


class Banks:
    def __init__(self, tiles, tag):
        self.t, self.i, self.tag = tiles, 0, tag

    def next(self):
        k = self.i % len(self.t)
        self.i += 1
        return (self.tag, k), self.t[k]


class Ctx:
    def __init__(self, nc, st, sc):
        self.nc, self.st, self.sc = nc, st, sc
        self.ev = 0

    def sb(self, name, shape, dt):
        return self.st.enter_context(self.nc.sbuf_tensor(f"{self.sc.name}_{name}", shape, dt))

    def ps(self, name, shape, dt):
        return self.st.enter_context(self.nc.psum_tensor(f"{self.sc.name}_{name}", shape, dt))

    def load(self, out, in_, reads, writes, slow=False):
        if slow:
            return self.sc.op("sp", lambda e: e.dma_start(out=out, in_=in_, allow_slow_non_contiguous=True),
                              reads, writes, dma=True)
        return self.sc.op("sp", lambda e: e.dma_start(out=out, in_=in_), reads, writes, dma=True)

    def store(self, out, in_, reads, writes):
        return self.sc.op("pool", lambda e: e.dma_start(out=out, in_=in_), reads, writes, dma=True)

    def mmgroup(self, out, pairs, reads, writes):
        n = len(pairs)

        def fn(e):
            inst = None
            for i, (l, r) in enumerate(pairs):
                inst = e.matmul(out, l, r, start=(i == 0), stop=(i == n - 1))
            return inst
        return self.sc.op("pe", fn, reads, writes)

    def mm(self, out, l, r, start, stop, reads, writes):
        return self.sc.op("pe", lambda e: e.matmul(out, l, r, start=start, stop=stop), reads, writes)

    def transposes(self, outs_ins, ident, reads, writes):
        def fn(e):
            inst = None
            for o, i in outs_ins:
                inst = e.transpose(o, i, ident)
            return inst
        return self.sc.op("pe", fn, reads, writes)

    def act(self, out, in_, func, reads, writes, bias=None, scale=None, accum_out=None):
        kw = {}
        if bias is not None:
            kw["bias"] = bias
        if scale is not None:
            kw["scale"] = scale
        if accum_out is not None:
            kw["accum_out"] = accum_out
        return self.sc.op("act", lambda e: e.activation(out, in_, func, **kw), reads, writes)

    def ts(self, out, in0, s1, s2, op0, op1, reads, writes, eng="dve"):
        if op1 is None:
            return self.sc.op(eng, lambda e: e.tensor_scalar(out, in0, s1, None, op0), reads, writes)
        return self.sc.op(eng, lambda e: e.tensor_scalar(out, in0, s1, s2, op0, op1), reads, writes)

    def tt(self, out, in0, in1, op, reads, writes, eng="dve"):
        return self.sc.op(eng, lambda e: e.tensor_tensor(out, in0, in1, op), reads, writes)

    def stt(self, out, in0, scalar, in1, op0, op1, reads, writes):
        return self.sc.op("dve", lambda e: e.scalar_tensor_tensor(out, in0, scalar, in1, op0, op1), reads, writes)

    def copy(self, out, in_, reads, writes, eng=None, scale=None):
        if eng is None:
            eng = "act" if (self.ev % 2 == 0) else "dve"
            self.ev += 1
        if eng == "act":
            if scale is None:
                return self.sc.op("act", lambda e: e.activation(out, in_, AF.Copy), reads, writes)
            return self.sc.op("act", lambda e: e.activation(out, in_, AF.Copy, scale=scale), reads, writes)
        if scale is None:
            return self.sc.op(eng, lambda e: e.tensor_copy(out, in_), reads, writes)
        return self.sc.op(eng, lambda e: e.tensor_scalar(out, in0=in_, scalar1=scale, scalar2=None, op0=ALU.mult),
                          reads, writes)

    def memset(self, ap, val, writes, eng="dve"):
        return self.sc.op(eng, lambda e: e.memset(ap, val), (), writes)

    def rstd(self, out, ss, inv_n, reads, writes):
        self.act(out, ss, AF.Ln, reads, writes, bias=EPS, scale=inv_n)
        self.act(out, out, AF.Exp, writes, writes, scale=-0.5)


def convert_weights(cx, Wb, wtag, w_dram, segs, gcol_fn, wst, ncols_max):
    k = 0
    nchunk = Wb.shape[1]
    for si, (dst, src, n) in enumerate(segs):
        for c in range(nchunk):
            slot = k % len(wst)
            k += 1
            cx.load(wst[slot][:, 0:n], w_dram[c * 128:(c + 1) * 128, src:src + n], (), [("wst", slot)])
            g = gcol_fn(c)
            if k % 2 == 0:
                cx.sc.op("act", lambda e, o=Wb[:, c, dst:dst + n], i=wst[slot][:, 0:n], g=g:
                         e.activation(o, i, AF.Copy, scale=g), [("wst", slot), "gains"], [(wtag, si, "a")])
            else:
                cx.sc.op("dve", lambda e, o=Wb[:, c, dst:dst + n], i=wst[slot][:, 0:n], g=g:
                         e.tensor_scalar(o, i, g, None, ALU.mult), [("wst", slot), "gains"], [(wtag, si, "d")])

    def regions(col):
        for si, (dst, src, n) in enumerate(segs):
            if dst <= col < dst + n:
                return [(wtag, si, "a"), (wtag, si, "d")]
        raise ValueError(col)
    return regions


def norm_a(cx, T, xs, ssq, rst, hb, junk, blk, slot):
    cx.load(xs[slot][:, :], T["xk"][blk * 128:(blk + 1) * 128, :], (), [("xs", slot)])
    cx.act(junk[:, :], xs[slot][:, :], AF.Square, [("xs", slot)], ["junk", ("ssq", slot)],
           accum_out=ssq[:, slot:slot + 1])
    cx.rstd(rst[:, slot:slot + 1], ssq[:, slot:slot + 1], 1.0 / D, [("ssq", slot)], [("rst", slot)])
    cx.ts(hb[:, :], xs[slot][:, :], rst[:, slot:slot + 1], None, ALU.mult, None,
          [("xs", slot), ("rst", slot)], ["hb"])


def norm_b(cx, hb, PT, ident, dst_hT, dst_region):
    cx.transposes([(PT[:, c * 128:(c + 1) * 128], hb[:, c * 128:(c + 1) * 128]) for c in range(8)],
                  ident[:, :], ["hb", "ident"], ["pt"])
    cx.copy(dst_hT, PT[:, :].rearrange("p (c t) -> p c t", c=8), ["pt"], [dst_region])


def norm_block(cx, T, xs, ssq, rst, hb, junk, PT, ident, blk, slot, dst_hT, dst_region, tag):
    norm_a(cx, T, xs, ssq, rst, hb, junk, blk, slot)
    norm_b(cx, hb, PT, ident, dst_hT, dst_region)


KA, VA, QB, KB, VB, GB, GK = 0, 1024, 2048, 2560, 3072, 4096, 5120
NW1 = 5136


def phase_A1(nc, T):
    with ExitStack() as st:
        sc = Sched(nc, "A1")
        cx = Ctx(nc, st, sc)
        Wb = cx.sb("Wb", [128, 8, NW1], BF16)
        wst = [cx.sb(f"wst{i}", [128, 1024], F32) for i in range(6)]
        gpre = cx.sb("gpre", [128, 8], F32)
        xs = [cx.sb(f"xs{i}", [128, D], F32) for i in range(2)]
        hb = cx.sb("hb", [128, D], BF16)
        junk = cx.sb("junk", [128, D], BF16)
        ssq = cx.sb("ssq", [128, 2], F32)
        rst = cx.sb("rst", [128, 2], F32)
        hT = [cx.sb(f"hT{i}", [128, 8, 512], BF16) for i in range(2)]
        stage_k = cx.sb("stage_k", [128, 8, 512], BF16)
        stage_v = cx.sb("stage_v", [128, 8, 512], BF16)
        mstage = cx.sb("mstage", [128, 8, 256], BF16)
        identf = cx.sb("identf", [128, 128], F32)
        ident = cx.sb("ident", [128, 128], BF16)
        trif = cx.sb("trif", [128, 128], F32)
        lstr = cx.sb("lstr", [128, 128], F32)
        tri4 = cx.sb("tri4", [128, 4, 128], BF16)
        onesf = cx.sb("onesf", [128, 2], F32)
        gkT = cx.sb("gkT", [32, 512], F32)
        waug = cx.sb("waug", [32, 512], F32)
        e_sb = cx.sb("e_sb", [128, 512], F32)
        l_sb = cx.sb("l_sb", [128, 512], F32)
        dt_sb = cx.sb("dt_sb", [128, 512], F32)
        dl_sb = cx.sb("dl_sb", [128, 4], F32)
        ktail = cx.sb("ktail", [128, 512], BF16)
        vb_bf = cx.sb("vb_bf", [128, 1024], BF16)
        state = cx.sb("state", [128, 1024], F32)
        state_bf = cx.sb("state_bf", [128, 1024], BF16)
        eneg = cx.sb("eneg", [128, 512], F32)
        epos = cx.sb("epos", [128, 512], F32)
        qdec = cx.sb("qdec", [128, 4, 128], BF16)
        kinv = cx.sb("kinv", [128, 4, 128], BF16)
        attn = cx.sb("attn", [128, 4, 128], BF16)
        sgb = cx.sb("sgb", [128, 1024], BF16)
        mixb = cx.sb("mixb", [128, 1024], BF16)
        ssq4 = cx.sb("ssq4", [128, 4], F32)
        rs4 = cx.sb("rs4", [128, 4], F32)
        pbs = Banks([cx.ps(f"pA{i}", [128, 512], F32) for i in range(7)], "ps")
        PT = cx.ps("PT", [128, 1024], BF16)

        cx.load(gpre[:, :], T["pre_g"].rearrange("o (c p) -> p (o c)", p=128), (), ["gains"], slow=True)
        cx.load(identf[:, :], T["ident"][:, :], (), ["identf"])
        cx.load(trif[:, :], T["tri"][:, :], (), ["trif"])
        cx.load(waug[0:16, :], T["w_gk"][:, :], (), ["waug"])
        cx.load(waug[16:17, :], T["b_gk"][:, :], (), ["waug"])
        cx.copy(ident[:, :], identf[:, :], ["identf"], ["ident"], eng="dve")
        cx.ts(lstr[:, :], trif[:, :], -1.0, 1.0, ALU.mult, ALU.add, ["trif"], ["lstr"])
        for h in range(4):
            cx.copy(tri4[:, h, :], trif[:, :], ["trif"], ["tri4"], eng="dve")
        cx.memset(onesf[:, :], 1.0, ["onesf"])
        cx.memset(gkT[:, :], 1.0, ["gkT"])
        cx.memset(state[:, :], 0.0, ["state"])
        cx.memset(state_bf[:, :], 0.0, ["state_bf"])
        segs = [(KA, 1024, 1024), (VA, 2048, 1024), (GK, 7168, 16), (QB, 4096, 1024), (VB, 5120, 1024),
                (GB, 6144, 1024)]
        for bi in range(4):
            norm_block(cx, T, xs, ssq, rst, hb, junk, PT, ident, bi, bi % 2,
                       hT[0][:, :, bi * 128:(bi + 1) * 128], ("hT", 0), "A1")
        WR = convert_weights(cx, Wb, "Wb", T["w_in"], segs, lambda c: gpre[:, c:c + 1], wst, 1040)

        NS = NBLK // 4

        def na(s, bi):
            blk = 4 * s + bi
            norm_a(cx, T, xs, ssq, rst, hb, junk, blk, blk % 2)

        def nb(s, bi):
            norm_b(cx, hb, PT, ident, hT[s % 2][:, :, bi * 128:(bi + 1) * 128], ("hT", s % 2))

        def norm_sb(s, bi):
            na(s, bi)
            nb(s, bi)

        deferred = []

        def flush():
            while deferred:
                deferred.pop(0)()

        for s in range(NS):
            hs = s % 2
            hTs = hT[hs]
            HR = ("hT", hs)
            nxt = s + 1 < NS
            if nxt:
                na(s + 1, 0)
            for h in range(8):
                pk, p = pbs.next()
                cx.mmgroup(p[:, :], [(Wb[:, c, KA + h * 128:KA + (h + 1) * 128], hTs[:, c, :]) for c in range(8)],
                           WR(KA) + [HR], [pk])
                cx.copy(stage_k[:, h, :], p[:, :], [pk], ["stage_k"])
                if h == 3:
                    flush()
            cx.store(T["KT"][:, :, s * 512:(s + 1) * 512].rearrange("h p n -> p h n"), stage_k[:, :, :],
                     ["stage_k"], ["KTd"])
            if nxt:
                nb(s + 1, 0)
                na(s + 1, 1)
            for bi in range(4):
                for half in range(2):
                    pk, p = pbs.next()
                    cx.mmgroup(p[:, :], [(hTs[:, c, bi * 128:(bi + 1) * 128],
                                          Wb[:, c, VA + half * 512:VA + (half + 1) * 512]) for c in range(8)],
                               WR(VA) + [HR], [pk])
                    cx.copy(stage_v[:, half * 4:(half + 1) * 4, bi * 128:(bi + 1) * 128],
                            p[:, :].rearrange("p (h v) -> p h v", h=4), [pk], ["stage_v"])
                if nxt and bi == 1:
                    nb(s + 1, 1)
            cx.store(T["Vs"][:, :, 4 * s:4 * s + 4, :].rearrange("h p i v -> p h (i v)"), stage_v[:, :, :],
                     ["stage_v"], ["Vd"])
            pk, p = pbs.next()
            cx.mmgroup(p[0:16, :], [(Wb[:, c, GK:GK + 16], hTs[:, c, :]) for c in range(8)], WR(GK) + [HR], [pk])
            cx.copy(gkT[0:16, :], p[0:16, :], [pk], ["gkT"], eng="dve")
            for bi in range(4):
                own = (bi % 2 == 1)
                oi = bi // 2
                tk = slice(bi * 128, (bi + 1) * 128)
                if nxt and own:
                    na(s + 1, 2 + oi)
                pk, p = pbs.next()
                cx.mm(p[:, :], gkT[0:17, tk], waug[0:17, :], True, True, ["gkT", "waug"], [pk])
                cx.act(e_sb[:, :], p[:, :], AF.Exp, [pk], ["e_sb"], scale=-1.0)
                cx.act(l_sb[:, :], e_sb[:, :], AF.Ln, ["e_sb"], ["l_sb"], bias=1.0)
                pkb, pkbt = pbs.next()
                cx.mmgroup(pkbt[:, :], [(hTs[:, c, tk], Wb[:, c, KB:KB + 512]) for c in range(8)], WR(KB) + [HR], [pkb])
                for half in range(2):
                    pk, p = pbs.next()
                    cx.mmgroup(p[:, :], [(hTs[:, c, tk], Wb[:, c, VB + half * 512:VB + (half + 1) * 512])
                                         for c in range(8)], WR(VB) + [HR], [pk])
                    cx.copy(vb_bf[:, half * 512:(half + 1) * 512], p[:, :], [pk], ["vb_bf"])
                flush()
                pk, p = pbs.next()
                cx.mm(p[:, :], lstr[:, :], l_sb[:, :], True, True, ["lstr", "l_sb"], [pk])
                cx.act(dt_sb[:, :], p[:, :], AF.Exp, [pk], ["dt_sb"], scale=-1.0 / 16.0)
                cx.tt(ktail[:, :], pkbt[:, :], dt_sb[:, :], ALU.mult, [pkb, "dt_sb"], ["ktail"])
                pk, p = pbs.next()
                for h in range(4):
                    cx.mm(p[:, h:h + 1], l_sb[:, h * 128:(h + 1) * 128], onesf[:, 0:1], True, True,
                          ["l_sb", "onesf"], [pk])
                cx.act(dl_sb[:, :], p[:, 0:4], AF.Exp, [pk], ["dl_sb"], scale=-1.0 / 16.0)
                if own:
                    pkc, pc = pbs.next()
                    for h in range(4):
                        cx.mm(pc[:, h * 128:(h + 1) * 128], l_sb[:, h * 128:(h + 1) * 128], trif[:, :], True, True,
                              ["l_sb", "trif"], [pkc])
                    cx.act(eneg[:, :], pc[:, :], AF.Exp, [pkc], ["eneg"], scale=-1.0 / 16.0)
                    cx.act(epos[:, :], pc[:, :], AF.Exp, [pkc], ["epos"], scale=1.0 / 16.0)
                    pkq, pq = pbs.next()
                    for h in range(4):
                        cx.mmgroup(pq[:, h * 128:(h + 1) * 128],
                                   [(Wb[:, c, QB + h * 128:QB + (h + 1) * 128], hTs[:, c, tk]) for c in range(8)],
                                   WR(QB) + [HR], [pkq])
                    cx.stt(qdec[:, :, :].rearrange("p h t -> p (h t)"), pq[:, :], 128.0 ** -0.5, eneg[:, :],
                           ALU.mult, ALU.mult, [pkq, "eneg"], ["qdec"])
                    pkk, pkt = pbs.next()
                    for h in range(4):
                        cx.mmgroup(pkt[:, h * 128:(h + 1) * 128],
                                   [(Wb[:, c, KB + h * 128:KB + (h + 1) * 128], hTs[:, c, tk]) for c in range(8)],
                                   WR(KB) + [HR], [pkk])
                    cx.tt(kinv[:, :, :].rearrange("p h t -> p (h t)"), pkt[:, :], epos[:, :], ALU.mult,
                          [pkk, "epos"], ["kinv"])
                    for half in range(2):
                        pk, p = pbs.next()
                        cx.mmgroup(p[:, :], [(hTs[:, c, tk], Wb[:, c, GB + half * 512:GB + (half + 1) * 512])
                                             for c in range(8)], WR(GB) + [HR], [pk])
                        cx.act(sgb[:, half * 512:(half + 1) * 512], p[:, :], AF.Silu, [pk], ["sgb"])
                    pka, pa = pbs.next()
                    for h in range(4):
                        cx.mm(pa[:, h * 128:(h + 1) * 128], kinv[:, h, :], qdec[:, h, :], True, True,
                              ["kinv", "qdec"], [pka])
                    cx.tt(attn[:, :, :].rearrange("p h t -> p (h t)"), pa[:, :],
                          tri4[:, :, :].rearrange("p h t -> p (h t)"), ALU.mult, [pka, "tri4"], ["attn"])
                pks = []
                for hp in range(2):
                    pk, p = pbs.next()
                    pks.append((pk, p))
                    for hh in range(2):
                        h = hp * 2 + hh
                        cx.mm(p[:, hh * 256:(hh + 1) * 256], ktail[:, h * 128:(h + 1) * 128],
                              vb_bf[:, h * 256:(h + 1) * 256], True, True, ["ktail", "vb_bf"], [pk])
                if own:
                    pko = []
                    for hp in range(2):
                        pk, p = pbs.next()
                        pko.append((pk, p))
                        for hh in range(2):
                            h = hp * 2 + hh
                            cx.mmgroup(p[:, hh * 256:(hh + 1) * 256],
                                       [(attn[:, h, :], vb_bf[:, h * 256:(h + 1) * 256]),
                                        (qdec[:, h, :], state_bf[:, h * 256:(h + 1) * 256])],
                                       ["attn", "vb_bf", "qdec", "state_bf"], [pk])
                    if nxt:
                        nb(s + 1, 2 + oi)
                    for h in range(4):
                        pk, p = pko[h // 2]
                        cx.act(junk[:, 0:256], p[:, (h % 2) * 256:(h % 2 + 1) * 256], AF.Square, [pk],
                               ["junk", "ssq4"], accum_out=ssq4[:, h:h + 1])
                    cx.rstd(rs4[:, :], ssq4[:, :], 1.0 / 256.0, ["ssq4"], ["rs4"])
                    for h in range(4):
                        pk, p = pko[h // 2]
                        cx.stt(mixb[:, h * 256:(h + 1) * 256], p[:, (h % 2) * 256:(h % 2 + 1) * 256],
                               rs4[:, h:h + 1], sgb[:, h * 256:(h + 1) * 256], ALU.mult, ALU.mult,
                               [pk, "rs4", "sgb"], ["mixb"])

                    def epilogue(oi=oi, s=s):
                        cx.transposes([(PT[:, c * 128:(c + 1) * 128], mixb[:, c * 128:(c + 1) * 128])
                                       for c in range(8)], ident[:, :], ["mixb", "ident"], ["pt"])
                        cx.copy(mstage[:, :, oi * 128:(oi + 1) * 128],
                                PT[:, :].rearrange("p (c t) -> p c t", c=8), ["pt"], ["mstage"])
                        if oi == 1:
                            t0 = 2 * s
                            cx.store(T["MT"][8:16, :, t0 * 128:(t0 + 2) * 128].rearrange("c p n -> p c n"),
                                     mstage[:, :, :], ["mstage"], ["MTd"])
                    deferred.append(epilogue)
                for hp in range(2):
                    pk, p = pks[hp]
                    for hh in range(2):
                        h = hp * 2 + hh
                        cx.stt(state[:, h * 256:(h + 1) * 256], state[:, h * 256:(h + 1) * 256], dl_sb[:, h:h + 1],
                               p[:, hh * 256:(hh + 1) * 256], ALU.mult, ALU.add, [pk, "dl_sb", "state"], ["state"])
                cx.copy(state_bf[:, :], state[:, :], ["state"], ["state_bf"], eng="act")
        flush()
        sc.run()


def phase_A2(nc, T):
    with ExitStack() as st:
        sc = Sched(nc, "A2")
        cx = Ctx(nc, st, sc)
        Wb = cx.sb("Wq", [128, 8, 2048], BF16)
        wst = [cx.sb(f"wst{i}", [128, 1024], F32) for i in range(6)]
        gpre = cx.sb("gpre", [128, 8], F32)
        xs = [cx.sb(f"xs{i}", [128, D], F32) for i in range(2)]
        hb = cx.sb("hb", [128, D], BF16)
        junk = cx.sb("junk", [128, D], BF16)
        ssq = cx.sb("ssq", [128, 2], F32)
        rst = cx.sb("rst", [128, 2], F32)
        hT = [cx.sb(f"hT{i}", [128, 8, 512], BF16) for i in range(2)]
        stq = [cx.sb(f"stq{i}", [128, 8, 512], BF16) for i in range(2)]
        identf = cx.sb("identf", [128, 128], F32)
        ident = cx.sb("ident", [128, 128], BF16)
        pbs = Banks([cx.ps(f"pB{i}", [128, 512], F32) for i in range(6)], "ps")
        PT = cx.ps("PT", [128, 1024], BF16)
        cx.load(gpre[:, :], T["pre_g"].rearrange("o (c p) -> p (o c)", p=128), (), ["gains"], slow=True)
        cx.load(identf[:, :], T["ident"][:, :], (), ["identf"])
        cx.copy(ident[:, :], identf[:, :], ["identf"], ["ident"], eng="dve")
        for bi in range(4):
            norm_block(cx, T, xs, ssq, rst, hb, junk, PT, ident, 2 * bi + 1, bi % 2,
                       hT[0][:, :, bi * 128:(bi + 1) * 128], ("hT", 0), "A2")
        WR = convert_weights(cx, Wb, "Wq", T["w_in"], [(0, 0, 1024), (1024, 3072, 1024)],
                             lambda c: gpre[:, c:c + 1], wst, 1024)
        NG2 = NOWN // 4

        def na2(g, bi):
            t = 4 * g + bi
            norm_a(cx, T, xs, ssq, rst, hb, junk, 2 * t + 1, t % 2)

        def nb2(g, bi):
            norm_b(cx, hb, PT, ident, hT[g % 2][:, :, bi * 128:(bi + 1) * 128], ("hT", g % 2))

        for g in range(NG2):
            hs = g % 2
            hTs = hT[hs]
            HR = ("hT", hs)
            nxt = g + 1 < NG2
            if nxt:
                na2(g + 1, 0)
            for which, dst, sreg in ((0, "QT", ("stq", 0)), (1, "GT", ("stq", 1))):
                for h in range(8):
                    pk, p = pbs.next()
                    col = which * 1024 + h * 128
                    cx.mmgroup(p[:, :], [(Wb[:, c, col:col + 128], hTs[:, c, :]) for c in range(8)], WR(col) + [HR], [pk])
                    if which == 0:
                        cx.copy(stq[0][:, h, :], p[:, :], [pk], [sreg], scale=0.125)
                    else:
                        cx.act(stq[1][:, h, :], p[:, :], AF.Silu, [pk], [sreg])
                    if nxt and h % 4 == 3:
                        j = which * 2 + h // 4
                        nb2(g + 1, j)
                        if j + 1 < 4:
                            na2(g + 1, j + 1)
                cx.store(T[dst][:, :, g * 512:(g + 1) * 512].rearrange("h p n -> p h n"), stq[which][:, :, :],
                         [sreg], [dst + "d"])
        sc.run()


def phase_B(nc, T):
    with ExitStack() as st:
        sc = Sched(nc, "B")
        cx = Ctx(nc, st, sc)
        NT = NBLK * 128
        NQ = NOWN * 128
        KTs = [cx.sb(f"KTs{i}", [128, NT], BF16) for i in range(2)]
        Vh = [cx.sb(f"Vh{i}", [128, NT], BF16) for i in range(2)]
        QTs = [cx.sb(f"QTs{i}", [128, NQ], BF16) for i in range(2)]
        GTs = [cx.sb(f"GTs{i}", [128, NQ], BF16) for i in range(2)]
        Pt = [[cx.sb(f"Pt{i}_{m}", [128, 512], BF16) for m in range(2)] for i in range(3)]
        acc1 = cx.sb("acc1", [128, 512], F32)
        trif = cx.sb("trif", [128, 128], F32)
        tri = cx.sb("tri", [128, 128], BF16)
        ones_bf = cx.sb("ones_bf", [128, 128], BF16)
        ones_f = cx.sb("ones_f", [128, 128], F32)
        kb0 = cx.sb("kb0", [128, 1], F32)
        lamt = cx.sb("lamt", [128, 256], F32)
        lprod = cx.sb("lprod", [128, 128], F32)
        lsum = cx.sb("lsum", [128, 2], F32)
        lexp = cx.sb("lexp", [128, 2], F32)
        neglam = cx.sb("neglam", [128, 1], F32)
        rc = [cx.sb(f"rc{m}", [128, 512], F32) for m in range(2)]
        nm = [cx.sb(f"nm{m}", [128, 512], F32) for m in range(2)]
        oa = cx.sb("oa", [128, 512], F32)
        sq = cx.sb("sq", [128, 512], F32)
        rsd = cx.sb("rsd", [128, 512], F32)
        mst = [cx.sb(f"mst{i}", [128, 512], BF16) for i in range(2)]
        sbk = Banks([cx.ps(f"pS{i}", [128, 512], F32) for i in range(4)], "ps")
        OT = [cx.ps(f"OT{m}", [128, 512], F32) for m in range(2)]
        DT = [cx.ps(f"DT{m}", [128, 512], F32) for m in range(2)]

        cx.load(trif[:, :], T["tri"][:, :], (), ["trif"])
        cx.load(kb0[:, :], T["kb0"][:, :], (), ["kb0"])
        cx.load(lamt[:, :], T["lam"].rearrange("a b -> (a b)").partition_broadcast(128), (), ["lamt"])
        cx.copy(tri[:, :], trif[:, :], ["trif"], ["tri"], eng="dve")
        cx.memset(ones_bf[:, :], 1.0, ["ones_bf"])
        cx.memset(ones_f[:, :], 1.0, ["ones_f"])
        cx.tt(lprod[:, 0:64], lamt[:, 0:64], lamt[:, 64:128], ALU.mult, ["lamt"], ["lprod"])
        cx.tt(lprod[:, 64:128], lamt[:, 128:192], lamt[:, 192:256], ALU.mult, ["lamt"], ["lprod"])
        cx.sc.op("dve", lambda e: e.reduce_sum(lsum[:, :], lprod[:, :].rearrange("p (a b) -> p a b", a=2), AX.X),
                 ["lprod"], ["lsum"])
        cx.act(lexp[:, :], lsum[:, :], AF.Exp, ["lsum"], ["lexp"])
        cx.tt(neglam[:, :], lexp[:, 1:2], lexp[:, 0:1], ALU.subtract, ["lexp"], ["neglam"])
        cx.ts(neglam[:, :], neglam[:, :], -LAM_INIT, None, ALU.add, None, ["neglam"], ["neglam"])

        OTs = [cx.sb(f"OTs{m}", [128, 512], F32) for m in range(2)]
        DTs = [cx.sb(f"DTs{m}", [128, 512], F32) for m in range(2)]
        NG = NOWN // 4
        steps = [(h, g, kb) for h in range(8) for g in range(NG) for kb in range(8 * g + 8)]

        def loads(h):
            sl = h % 2
            for q in range(4):
                cs = slice(q * (NT // 4), (q + 1) * (NT // 4))
                cx.load(KTs[sl][:, cs], T["KT"][h, :, cs], (), [("KTs", sl)])
                cx.load(Vh[sl][:, cs], T["Vs"][h].rearrange("p i v -> p (i v)")[:, cs], (), [("Vh", sl)])
            cx.load(QTs[sl][:, :], T["QT"][h, :, :], (), [("QTs", sl)])
            cx.load(GTs[sl][:, :], T["GT"][h, :, :], (), [("GTs", sl)])

        def geom(g, kb):
            d = kb - (8 * g + 1)
            qs = 0 if d <= 0 else (d + 1) // 2
            return qs, (d >= 0 and d % 2 == 0), slice(qs * 128, 512)

        def front(i):
            h, g, kb = steps[i]
            sl = h % 2
            qs, diag, cols = geom(g, kb)
            for m in range(2):
                P = Pt[i % 3][m]
                PR = ("Pt", i % 3, m)
                pk, p = sbk.next()
                rows = slice(m * 64, (m + 1) * 64)
                cx.mm(p[:, cols], KTs[sl][rows, kb * 128:(kb + 1) * 128],
                      QTs[sl][rows, g * 512 + qs * 128:(g + 1) * 512], True, True,
                      [("KTs", sl), ("QTs", sl)], [pk])
                if kb == 0:
                    cx.act(P[:, cols], p[:, cols], AF.Exp, [pk, "kb0"], [PR], bias=kb0[:, 0:1])
                else:
                    cx.act(P[:, cols], p[:, cols], AF.Exp, [pk], [PR])
                if diag:
                    dc = slice(qs * 128, (qs + 1) * 128)
                    cx.tt(P[:, dc], P[:, dc], tri[:, :], ALU.mult, [PR, "tri"], [PR])

        def back(i):
            h, g, kb = steps[i]
            sl = h % 2
            nkb = 8 * g + 8
            qs, diag, cols = geom(g, kb)
            for m in range(2):
                P = Pt[i % 3][m]
                PR = ("Pt", i % 3, m)
                cx.mm(OT[m][:, cols], Vh[sl][:, kb * 128:(kb + 1) * 128], P[:, cols],
                      kb == 0, kb == nkb - 1, [("Vh", sl), PR], [("OT", m)])
                if m == 0:
                    cx.mm(DT[m][:, cols], ones_bf[:, :], P[:, cols],
                          kb == 0, kb == nkb - 1, ["ones_bf", PR], [("DT", m)])
                elif kb == 0:
                    cx.copy(acc1[:, cols], P[:, cols], [PR], ["acc1"], eng="dve")
                else:
                    cx.tt(acc1[:, cols], acc1[:, cols], P[:, cols], ALU.add, [PR, "acc1"], ["acc1"])

        gcount = [0]

        def gend_now(h, g):
            cx.mm(DT[1][:, :], ones_f[:, :], acc1[:, :], True, True, ["ones_f", "acc1"], [("DT", 1)])
            for m in range(2):
                cx.copy(DTs[m][:, :], DT[m][:, :], [("DT", m)], [("DTs", m)], eng="dve")
                cx.copy(OTs[m][:, :], OT[m][:, :], [("OT", m)], [("OTs", m)], eng="dve")

        def gend_ops(h, g):
            sl = h % 2
            ops = []
            for m in range(2):
                for q in range(4):
                    c = slice(q * 128, (q + 1) * 128)
                    ops.append(lambda m=m, c=c: cx.sc.op(
                        "dve", lambda e, o=rc[m][:, c], i_=DTs[m][:, c]: e.reciprocal(o, i_),
                        [("DTs", m)], [("rc", m)]))
                ops.append(lambda m=m: cx.tt(nm[m][:, :], OTs[m][:, :], rc[m][:, :], ALU.mult,
                                             [("OTs", m), ("rc", m)], [("nm", m)], eng="pool"))
            ops.append(lambda: cx.stt(oa[:, :], nm[1][:, :], neglam[:, 0:1], nm[0][:, :], ALU.mult, ALU.add,
                                      [("nm", 0), ("nm", 1), "neglam"], ["oa"]))
            ops.append(lambda: cx.tt(sq[:, :], oa[:, :], oa[:, :], ALU.mult, ["oa"], ["sq"], eng="pool"))
            ops.append(None)
            ops.append(None)

            def norm_mm():
                pk, p = sbk.next()
                cx.mm(p[:, :], ones_f[:, :], sq[:, :], True, True, ["ones_f", "sq"], [pk])
                cx.rstd(rsd[:, :], p[:, :], 1.0 / 128.0, [pk], ["rsd"])
            ops.append(norm_mm)
            ops.append(None)
            ops.append(lambda: cx.tt(oa[:, :], oa[:, :], rsd[:, :], ALU.mult, ["oa", "rsd"], ["oa"], eng="pool"))

            def fin():
                ms = gcount[0] % 2
                gcount[0] += 1
                cx.tt(mst[ms][:, :], oa[:, :], GTs[sl][:, g * 512:(g + 1) * 512], ALU.mult,
                      ["oa", ("GTs", sl)], [("mst", ms)], eng="pool")
                cx.store(T["MT"][h, :, g * 512:(g + 1) * 512], mst[ms][:, :], [("mst", ms)], ["MTd"])
            ops.append(fin)
            return ops

        pending = []
        loads(0)
        loads(1)
        front(0)
        for i in range(len(steps)):
            h, g, kb = steps[i]
            if i + 1 < len(steps):
                if steps[i + 1][0] != h:
                    while pending:
                        pending.pop(0)[1]()
                front(i + 1)
            back(i)
            while pending and pending[0][0] <= i:
                pending.pop(0)[1]()
            if kb == 8 * g + 7:
                while pending:
                    pending.pop(0)[1]()
                gend_now(h, g)
                k = 0
                for fn in gend_ops(h, g):
                    k += 1
                    if fn is not None:
                        pending.append((i + k, fn))
                if g == NG - 1 and h + 2 < 8:
                    pending.append((i + k, lambda h=h: loads(h + 2)))
        while pending:
            pending.pop(0)[1]()
        sc.run()


def phase_C(nc, T):
    with ExitStack() as st:
        sc = Sched(nc, "C")
        cx = Ctx(nc, st, sc)
        Wo = cx.sb("Wo", [128, 16, 1024], BF16)
        wst = [cx.sb(f"wst{i}", [128, 1024], F32) for i in range(6)]
        gsub = cx.sb("gsub", [128, 1], F32)
        ggla = cx.sb("ggla", [128, 2], F32)
        postg = cx.sb("postg", [128, D], F32)
        mixT = [cx.sb(f"mixT{i}", [128, 16, 512], BF16) for i in range(2)]
        xr = [cx.sb(f"xr{i}", [128, D], F32) for i in range(2)]
        ot = [cx.sb(f"ot{i}", [128, D], F32) for i in range(2)]
        junk = cx.sb("junk", [128, 512], BF16)
        ss2 = cx.sb("ss2", [128, 2], F32)
        ss1 = cx.sb("ss1", [128, 1], F32)
        rs1 = cx.sb("rs1", [128, 1], F32)
        pbs = Banks([cx.ps(f"pC{i}", [128, 512], F32) for i in range(6)], "ps")

        cx.load(gsub[:, :], T["subln_g"].rearrange("o p -> p o"), (), ["gains"], slow=True)
        cx.load(ggla[:, :], T["gla_g"].rearrange("o (c p) -> p (o c)", p=128), (), ["gains"], slow=True)
        cx.load(postg[:, :], T["post_g"].rearrange("o d -> (o d)").partition_broadcast(128), (), ["postg"])
        cx.ts(gsub[:, :], gsub[:, :], 1.0 - LAM_INIT, None, ALU.mult, None, ["gains"], ["gains"])
        def load_mix(g):
            for half in range(2):
                cx.load(mixT[g % 2][:, half * 8:(half + 1) * 8, :],
                        T["MT"][half * 8:(half + 1) * 8, :, g * 512:(g + 1) * 512].rearrange("c p n -> p c n"),
                        (), [("mixT", g % 2)])

        load_mix(0)
        WR = convert_weights(cx, Wo, "Wo", T["w_out"], [(0, 0, 512), (512, 512, 512)],
                             lambda c: (gsub[:, 0:1] if c < 8 else ggla[:, (c % 2):(c % 2) + 1]), wst, 1024)
        for g in range(NOWN // 4):
            sl = g % 2
            MR = ("mixT", sl)
            if g + 1 < NOWN // 4 and g >= 1:
                load_mix(g + 1)
            if g == 0 and NOWN // 4 > 1:
                load_mix(1)
            for tb in range(4):
                t = 4 * g + tb
                xsl = t % 2
                cx.load(xr[xsl][:, :], T["xk"][(2 * t + 1) * 128:(2 * t + 2) * 128, :], (), [("xr", xsl)])
                pks = []
                for half in range(2):
                    pk, p = pbs.next()
                    pks.append((pk, p))
                    cx.mmgroup(p[:, :], [(mixT[sl][:, c, tb * 128:(tb + 1) * 128],
                                          Wo[:, c, half * 512:(half + 1) * 512]) for c in range(16)],
                               WR(half * 512) + [MR], [pk])
                    cx.act(junk[:, :], p[:, :], AF.Square, [pk], ["junk", "ss2"], accum_out=ss2[:, half:half + 1])
                cx.tt(ss1[:, :], ss2[:, 0:1], ss2[:, 1:2], ALU.add, ["ss2"], ["ss1"])
                cx.rstd(rs1[:, :], ss1[:, :], 1.0 / D, ["ss1"], ["rs1"])
                OR = ("ot", xsl)
                for half in range(2):
                    pk, p = pks[half]
                    hc = slice(half * 512, (half + 1) * 512)
                    cx.stt(ot[xsl][:, hc], p[:, :], rs1[:, 0:1], postg[:, hc], ALU.mult, ALU.mult,
                           [pk, "rs1", "postg"], [OR])
                cx.tt(ot[xsl][:, :], ot[xsl][:, :], xr[xsl][:, :], ALU.add, [OR, ("xr", xsl)], [OR])
                cx.store(T["out"][t * 128:(t + 1) * 128, :], ot[xsl][:, :], [OR], ["outd"])
        sc.run()


def build_program():
    nc = bass.Bass("TRN2", target_bir_lowering=False)
    T = {}

    def din(name, shape):
        T[name] = nc.dram_tensor(name, shape, F32, kind="ExternalInput").ap()

    din("xk", [NBLK * 128, D])
    din("w_in", [D, 7184])
    din("w_out", [2048, D])
    din("w_gk", [16, 512])
    din("b_gk", [1, 512])
    din("pre_g", [1, D])
    din("post_g", [1, D])
    din("subln_g", [1, 128])
    din("gla_g", [1, 256])
    din("lam", [4, 64])
    din("kb0", [128, 1])
    din("tri", [128, 128])
    din("ident", [128, 128])
    T["out"] = nc.dram_tensor("out", [NOWN * 128, D], F32, kind="ExternalOutput").ap()
    T["KT"] = nc.dram_tensor("KT", [8, 128, NBLK * 128], BF16, kind="Internal").ap()
    T["Vs"] = nc.dram_tensor("Vs", [8, 128, NBLK, 128], BF16, kind="Internal").ap()
    T["QT"] = nc.dram_tensor("QT", [8, 128, NOWN * 128], BF16, kind="Internal").ap()
    T["GT"] = nc.dram_tensor("GT", [8, 128, NOWN * 128], BF16, kind="Internal").ap()
    T["MT"] = nc.dram_tensor("MT", [16, 128, NOWN * 128], BF16, kind="Internal").ap()
    phase_A1(nc, T)
    phase_A2(nc, T)
    phase_B(nc, T)
    phase_C(nc, T)
    return nc


def kernel(x, pre_norm_g, post_norm_g, w_in, w_gk_up, b_gk, lambda_q1, lambda_k1, lambda_q2, lambda_k2,
           attn_subln_g, gla_norm_g, w_out):
    f = np.float32
    x = np.asarray(x, f)
    B = x.shape[0]
    common = {
        "w_in": np.ascontiguousarray(np.asarray(w_in, f)[0]),
        "w_out": np.ascontiguousarray(np.asarray(w_out, f)[0]),
        "w_gk": np.ascontiguousarray(np.asarray(w_gk_up, f)[0]),
        "b_gk": np.asarray(b_gk, f).reshape(1, 512),
        "pre_g": np.asarray(pre_norm_g, f).reshape(1, D),
        "post_g": np.asarray(post_norm_g, f).reshape(1, D),
        "subln_g": np.asarray(attn_subln_g, f).reshape(1, 128),
        "gla_g": np.asarray(gla_norm_g, f).reshape(1, 256),
        "lam": np.stack([np.asarray(v, f).reshape(64) for v in (lambda_q1, lambda_k1, lambda_q2, lambda_k2)]),
        "tri": np.triu(np.ones((128, 128), f)),
        "ident": np.eye(128, dtype=f),
    }
    in_maps = []
    for b in range(B):
        xa = np.concatenate([np.zeros((128, D), f), x[b, :63 * 128]], axis=0)
        in_maps.append(dict(common, xk=np.ascontiguousarray(xa), kb0=np.full((128, 1), NEG, f)))
        in_maps.append(dict(common, xk=np.ascontiguousarray(x[b]), kb0=np.zeros((128, 1), f)))
    nc = build_program()
    res = run_bass_kernel_spmd(nc, in_maps, core_ids=list(range(2 * B)))
    out = np.empty((B, 64, 128, D), f)
    for b in range(B):
        out[b, 0::2] = np.asarray(res.results[2 * b]["out"], f).reshape(NOWN, 128, D)
        out[b, 1::2] = np.asarray(res.results[2 * b + 1]["out"], f).reshape(NOWN, 128, D)
    return out.reshape(B, 64 * 128, D)
```
